# Optimizing a Trainium2 kernel written in Bass

```python
import math
import jax, jax.numpy as jnp
from jax import lax
import numpy as np

D_MODEL = 1024
BATCH = 16
SEQ = 2048
DEPTH = 2

EPS = 1e-6
POOL_WINDOWS = (2, 4, 8, 16)
POOL_GROUPS = 4
POOL_DH = D_MODEL // 8
POOL_WIDTH = POOL_GROUPS * POOL_DH
CONV_WIDTH = D_MODEL // 2
CONV_K = 3
AB_IN = POOL_WIDTH + 3 * CONV_WIDTH
AB_OUT = POOL_WIDTH + CONV_WIDTH
SGU_CHUNK = 128
SGU_GROUPS = 4
SGU_DH = D_MODEL // 8
SGU_WIDTH = SGU_GROUPS * SGU_DH
SB_HEADS = 8
SB_DH = 64
SB_WIDTH = SB_HEADS * SB_DH
SB_BLOCK = 128
CD_IN = 2 * SGU_WIDTH + 3 * SB_WIDTH
CD_OUT = SGU_WIDTH + SB_WIDTH
D_FF = 4 * D_MODEL
N_AB = (DEPTH + 1) // 2
N_CD = DEPTH // 2

kernel_name = 'hybrid_pool_conv_sgu_stickbreak_trunk'


def rmsnorm(x, g):
    xf = x.astype(jnp.float32)
    y = xf * lax.rsqrt(jnp.mean(xf * xf, axis=-1, keepdims=True) + EPS)
    return (y * g.astype(jnp.float32)).astype(x.dtype)


def layernorm(x, g, b):
    xf = x.astype(jnp.float32)
    mu = jnp.mean(xf, axis=-1, keepdims=True)
    xc = xf - mu
    y = xc * lax.rsqrt(jnp.mean(xc * xc, axis=-1, keepdims=True) + EPS)
    return (y * g.astype(jnp.float32) + b.astype(jnp.float32)).astype(x.dtype)


def pool_mixer(a, w, scale):
    T = a.shape[1]
    af = a.astype(jnp.float32)
    cs = jnp.pad(jnp.cumsum(af, axis=1), ((0, 0), (1, 0), (0, 0), (0, 0)))
    win = jnp.array(POOL_WINDOWS, dtype=jnp.int32)
    pos = jnp.arange(T, dtype=jnp.int32)
    start = jnp.maximum(pos[:, None] + 1 - win[None, :], 0)
    count = jnp.minimum(pos[:, None] + 1, win[None, :]).astype(jnp.float32)
    grp = jnp.arange(POOL_GROUPS, dtype=jnp.int32)
    window_sum = cs[:, 1:] - cs[:, start, grp[None, :], :]
    pooled = (window_sum / count[None, :, :, None] - af).astype(a.dtype)
    mixed = jnp.einsum('btgc,gcd->btgd', pooled, w)
    return mixed * scale


def short_conv(h, w, b):
    T = h.shape[1]
    hp = jnp.pad(h, ((0, 0), (CONV_K - 1, 0), (0, 0)))
    y = hp[:, 0:T] * w[0]
    for k in range(1, CONV_K):
        y = y + hp[:, k:k + T] * w[k]
    return y + b


def spatial_gating(u, v, g, beta, w_s, b_s):
    B, T, _ = v.shape
    v = layernorm(v, g, beta)
    vc = v.reshape(B, T // SGU_CHUNK, SGU_CHUNK, SGU_GROUPS, SGU_DH)
    causal = jnp.tril(jnp.ones((SGU_CHUNK, SGU_CHUNK), dtype=bool))
    w = jnp.where(causal[None], w_s, 0)
    s = jnp.einsum('gts,bnsgc->bntgc', w, vc) + b_s.T[:, :, None]
    return u * s.reshape(B, T, SGU_WIDTH)


def stick_breaking_attention(q, k, v):
    B, T, H, dh = q.shape
    q = q.transpose(0, 2, 1, 3)
    k = k.transpose(0, 2, 1, 3)
    v = v.transpose(0, 2, 1, 3)
    scale = 1.0 / math.sqrt(dh)
    outs = []
    for i in range(T // SB_BLOCK):
        q0 = i * SB_BLOCK
        kend = q0 + SB_BLOCK
        qb = q[:, :, q0:kend]
        kb = k[:, :, :kend]
        vb = v[:, :, :kend]
        z = jnp.einsum('bhqd,bhkd->bhqk', qb, kb,
                       preferred_element_type=jnp.float32) * scale
        qpos = q0 + jnp.arange(SB_BLOCK, dtype=jnp.int32)
        kpos = jnp.arange(kend, dtype=jnp.int32)
        mask = kpos[None, :] < qpos[:, None]
        log_keep = jnp.where(mask, jax.nn.log_sigmoid(-z), 0.0)
        suffix = lax.cumsum(log_keep, axis=3, reverse=True) - log_keep
        weights = jnp.where(mask, jnp.exp(jax.nn.log_sigmoid(z) + suffix), 0.0)
        outs.append(jnp.einsum('bhqk,bhkd->bhqd', weights.astype(vb.dtype), vb))
    o = jnp.concatenate(outs, axis=2)
    return o.transpose(0, 2, 1, 3).reshape(B, T, H * dh)


def setup_inputs(seed: int = 0) -> dict:
    key = jax.random.key(seed)
    ks = jax.random.split(key, 20)
    nrm = jax.random.normal
    f32 = jnp.float32
    res_scale = 1.0 / math.sqrt(2 * DEPTH)
    return {
        'x': nrm(ks[0], (BATCH, SEQ, D_MODEL), f32),
        'mix_norm_g': 1.0 + 0.01 * nrm(ks[1], (DEPTH, D_MODEL), f32),
        'mlp_norm_g': 1.0 + 0.01 * nrm(ks[2], (DEPTH, D_MODEL), f32),
        'ab_w_in': nrm(ks[3], (N_AB, D_MODEL, AB_IN), f32) * D_MODEL ** -0.5,
        'pool_w': nrm(ks[4], (N_AB, POOL_GROUPS, POOL_DH, POOL_DH), f32) * POOL_DH ** -0.5,
        'pool_scale': 1.0 + 0.02 * nrm(ks[5], (N_AB, POOL_GROUPS, POOL_DH), f32),
        'conv_w': nrm(ks[6], (N_AB, CONV_K, CONV_WIDTH), f32) * CONV_K ** -0.5,
        'conv_b': 0.01 * nrm(ks[7], (N_AB, CONV_WIDTH), f32),
        'ab_w_out': nrm(ks[8], (N_AB, AB_OUT, D_MODEL), f32) * AB_OUT ** -0.5 * res_scale,
        'cd_w_in': nrm(ks[9], (N_CD, D_MODEL, CD_IN), f32) * D_MODEL ** -0.5,
        'sgu_norm_g': 1.0 + 0.01 * nrm(ks[10], (N_CD, SGU_WIDTH), f32),
        'sgu_norm_b': 0.01 * nrm(ks[11], (N_CD, SGU_WIDTH), f32),
        'sgu_w': nrm(ks[12], (N_CD, SGU_GROUPS, SGU_CHUNK, SGU_CHUNK), f32) * SGU_CHUNK ** -0.5,
        'sgu_b': 1.0 + 0.01 * nrm(ks[13], (N_CD, SGU_GROUPS, SGU_CHUNK), f32),
        'cd_w_out': nrm(ks[14], (N_CD, CD_OUT, D_MODEL), f32) * CD_OUT ** -0.5 * res_scale,
        'mlp_w1': nrm(ks[15], (DEPTH, D_MODEL, D_FF), f32) * D_MODEL ** -0.5,
        'mlp_w2': nrm(ks[16], (DEPTH, D_FF, D_MODEL), f32) * D_FF ** -0.5 * res_scale,
        'final_norm_g': 1.0 + 0.01 * nrm(ks[17], (D_MODEL,), f32),
    }


def reference(x, mix_norm_g, mlp_norm_g, ab_w_in, pool_w, pool_scale, conv_w, conv_b,
              ab_w_out, cd_w_in, sgu_norm_g, sgu_norm_b, sgu_w, sgu_b, cd_w_out,
              mlp_w1, mlp_w2, final_norm_g):
    B, T, _ = x.shape
    h = x
    for layer in range(DEPTH):
        xn = rmsnorm(h, mix_norm_g[layer])
        if layer % 2 == 0:
            i = layer // 2
            p = xn @ ab_w_in[i]
            a, xb, gate_b, gate_c = jnp.split(
                p, [POOL_WIDTH, POOL_WIDTH + CONV_WIDTH, POOL_WIDTH + 2 * CONV_WIDTH], axis=-1)
            a_out = pool_mixer(a.reshape(B, T, POOL_GROUPS, POOL_DH),
                               pool_w[i], pool_scale[i]).reshape(B, T, POOL_WIDTH)
            b_out = gate_b * short_conv(gate_c * xb, conv_w[i], conv_b[i])
            mix = jnp.concatenate([a_out, b_out], axis=-1) @ ab_w_out[i]
        else:
            i = layer // 2
            p = xn @ cd_w_in[i]
            uv = jax.nn.gelu(p[..., :2 * SGU_WIDTH], approximate=False)
            u, v = jnp.split(uv, 2, axis=-1)
            c_out = spatial_gating(u, v, sgu_norm_g[i], sgu_norm_b[i], sgu_w[i], sgu_b[i])
            qkv = p[..., 2 * SGU_WIDTH:].reshape(B, T, 3, SB_HEADS, SB_DH)
            d_out = stick_breaking_attention(qkv[:, :, 0], qkv[:, :, 1], qkv[:, :, 2])
            mix = jnp.concatenate([c_out, d_out], axis=-1) @ cd_w_out[i]
        h = h + mix
        hn = rmsnorm(h, mlp_norm_g[layer])
        h = h + jnp.square(jax.nn.relu(hn @ mlp_w1[layer])) @ mlp_w2[layer]
    return rmsnorm(h, final_norm_g)
```

```python
import contextlib
import numpy as np
import concourse.bass as bass
import concourse.mybir as mybir
from concourse.bass_utils import run_bass_kernel_spmd

F32 = mybir.dt.float32
BF16 = mybir.dt.bfloat16
AF = mybir.ActivationFunctionType
ALU = mybir.AluOpType

NCORES = 8
SEQ = 2048
D = 1024
NSEQ = 2
EPS = 1e-6
RING = 5
NEG = -30000.0

ENGS = ("pe", "act", "dve", "pool", "sp")


class LT:
    __slots__ = ("name", "w", "rs")

    def __init__(self, name):
        self.name = name
        self.w = None
        self.rs = []


class Op:
    __slots__ = ("eng", "fn", "waits", "dma", "chan", "val", "sig", "vc", "idx", "val2")


class Sched:
    def __init__(self, nc):
        self.nc = nc
        self.ops = {e: [] for e in ENGS}
        self.cur = {e: {} for e in ENGS}
        self.dma_gen = {}

    def alias(self, new_lts, old_lts):
        pend = []
        for t in old_lts:
            if t.w is not None:
                pend.append(t.w)
            pend.extend(t.rs)
        for t in new_lts:
            t.rs = list(t.rs) + pend

    def op(self, eng, fn, reads=(), writes=(), dma_key=None):
        o = Op()
        o.eng = eng
        o.fn = fn
        o.dma = dma_key is not None
        o.sig = o.dma
        o.idx = len(self.ops[eng])
        deps = []
        for t in reads:
            if t.w is not None:
                deps.append(t.w)
        for t in writes:
            if t.w is not None:
                deps.append(t.w)
            deps.extend(t.rs)
        cur = self.cur[eng]
        waits = []
        for d in deps:
            if (not d.dma) and (not o.dma) and d.eng == "pe" and eng == "pe":
                continue
            if cur.get(d.chan, -1) >= d.val:
                continue
            waits.append(d)
            d.sig = True
            for k, v in d.vc.items():
                if cur.get(k, -1) < v:
                    cur[k] = v
        best = {}
        for d in waits:
            if d.chan not in best or best[d.chan].val < d.val:
                best[d.chan] = d
        o.waits = list(best.values())
        if o.dma:
            g = self.dma_gen.get(dma_key, 0) + 1
            self.dma_gen[dma_key] = g
            o.chan = ("dma", dma_key)
            o.val = g
        else:
            o.chan = eng
            o.val = o.idx
        vc = dict(cur)
        vc[o.chan] = o.val
        o.vc = vc
        for t in reads:
            t.rs.append(o)
        for t in writes:
            t.w = o
            t.rs = []
        self.ops[eng].append(o)
        return o

    def emit(self, final_waits=()):
        nc = self.nc
        for e in ENGS:
            c = 0
            for o in self.ops[e]:
                if o.dma:
                    continue
                if o.sig:
                    c += 1
                    o.val2 = c
        dma_keys = list(self.dma_gen.keys())
        with contextlib.ExitStack() as st:
            esem = {e: st.enter_context(nc.semaphore("s_" + e)) for e in ENGS}
            dsem = {k: st.enter_context(nc.semaphore("d_%d" % i)) for i, k in enumerate(dma_keys)}
            block = st.enter_context(nc.Block())

            def semval(d):
                if d.dma:
                    return dsem[d.chan[1]], 16 * d.val
                return esem[d.eng], d.val2

            def run(engname):
                def body(eng):
                    for o in self.ops[engname]:
                        for d in o.waits:
                            s, v = semval(d)
                            eng.wait_ge(s, v)
                        ins = o.fn(eng)
                        if o.sig:
                            if o.dma:
                                ins.then_inc(dsem[o.chan[1]], 16)
                            else:
                                ins.then_inc(esem[engname], 1)
                    if engname == "sp":
                        for d in final_waits:
                            s, v = semval(d)
                            eng.wait_ge(s, v)
                return body

            block.tensor(run("pe"))
            block.scalar(run("act"))
            block.vector(run("dve"))
            block.gpsimd(run("pool"))
            block.sync(run("sp"))


def _blk(W, col0):
    return np.ascontiguousarray(
        W[:, col0:col0 + 128].reshape(8, 128, 128).transpose(1, 0, 2)).reshape(128, 1024)


def _mov(W, col0):
    return np.ascontiguousarray(
        W[:, col0:col0 + 512].reshape(8, 128, 512).transpose(1, 0, 2)).reshape(128, 4096)


L0_OC_ORDER = [0, 1, 2, 3] + [o for cc in range(4) for o in (4 + cc, 12 + cc, 8 + cc)]


def _prep_weights(inp):
    blocks = []
    w = inp["ab_w_in"][0]
    for oc in L0_OC_ORDER:
        blocks.append(_blk(w, oc * 128))
    w = inp["ab_w_out"][0]
    for c in range(8):
        blocks.append(_blk(w, c * 128))

    def mlp_blocks(l):
        w1 = inp["mlp_w1"][l]
        w2 = inp["mlp_w2"][l]
        for q in range(4):
            for j in range(8):
                blocks.append(_blk(w1, (q * 8 + j) * 128))
            for c in range(8):
                blocks.append(_blk(w2[q * 1024:(q + 1) * 1024], c * 128))

    mlp_blocks(0)
    w = inp["cd_w_in"][0]
    for g in range(4):
        blocks.append(_blk(w, g * 128))
    for a in range(4):
        blocks.append(_blk(w, 1024 + a * 128))
        blocks.append(_blk(w, 1536 + a * 128))
    wo = inp["cd_w_out"][0]
    for c in range(8):
        blocks.append(_blk(wo, c * 128))
    mlp_blocks(1)
    wblk = np.ascontiguousarray(np.stack(blocks, 0)).astype(np.float32, copy=False)
    wmov = np.ascontiguousarray(np.stack([_mov(w, 512), _mov(w, 2048)], 0))

    def col8(v):
        return v.reshape(8, 128).T

    gv = np.zeros((128, 64), np.float32)
    gv[:, 0:8] = col8(inp["mix_norm_g"][0])
    gv[:, 8:16] = col8(inp["mlp_norm_g"][0])
    gv[:, 16:24] = col8(inp["mix_norm_g"][1])
    gv[:, 24:32] = col8(inp["mlp_norm_g"][1])
    gv[:, 32:40] = col8(inp["final_norm_g"])
    gv[:, 40:44] = inp["pool_scale"][0].T
    gv[:, 44:56] = inp["conv_w"][0].reshape(3, 4, 128).transpose(2, 0, 1).reshape(128, 12)
    gv[:, 56:60] = inp["conv_b"][0].reshape(4, 128).T
    rc = np.zeros((128, 4, 16), np.float32)
    for g, wdw in enumerate((2, 4, 8, 16)):
        rc[:, g, :] = 1.0 / np.minimum(np.arange(16) + 1, wdw)
    cf = np.concatenate([
        gv, rc.reshape(128, 64),
        np.tile(inp["sgu_norm_g"][0].reshape(1, 512), (128, 1)),
        np.tile(inp["sgu_norm_b"][0].reshape(1, 512), (128, 1)),
        np.tile(inp["sgu_b"][0].reshape(1, 512), (128, 1)),
        np.full((128, 512), -0.5, np.float32),
    ], axis=1).astype(np.float32)
    j = np.arange(128)[:, None]
    k = np.arange(128)[None, :]
    ones = np.ones((128, 128), np.float32)
    ident = (j == k).astype(np.float32)
    nui = -(j >= k).astype(np.float32)
    nones = -ones
    negm = np.where(j >= k, NEG, 0.0).astype(np.float32)
    poolw = inp["pool_w"][0].transpose(1, 0, 2).reshape(128, 512)
    cb = np.concatenate([ones, ident, nui, nones, negm, poolw], axis=1).astype(np.float32)
    wT = inp["sgu_w"][0].transpose(2, 0, 1).reshape(128, 512)
    tril = np.tile((j <= k).astype(np.float32), (1, 4))
    sg = np.concatenate([wT, tril], axis=1).astype(np.float32)
    return dict(wblk=wblk, wmov=wmov, cf=cf, cb=cb, sg=sg)


NBLK = 172


class Builder:
    def __init__(self, nseq=NSEQ, layers=(0, 1), debug=False):
        self.nseq = nseq
        self.layers = layers
        nc = self.nc = bass.Bass("TRN2", target_bir_lowering=False)
        self.S = Sched(nc)
        self.xT = nc.dram_tensor("xT", [nseq, D, SEQ], F32, kind="ExternalInput").ap()
        self.wblk = nc.dram_tensor("wblk", [NBLK, 128, 1024], F32, kind="ExternalInput").ap()
        self.wmov = nc.dram_tensor("wmov", [2, 128, 4096], F32, kind="ExternalInput").ap()
        self.cf_d = nc.dram_tensor("cf", [128, 2176], F32, kind="ExternalInput").ap()
        self.cb_d = nc.dram_tensor("cb", [128, 1152], F32, kind="ExternalInput").ap()
        self.sg_d = nc.dram_tensor("sg", [128, 1024], F32, kind="ExternalInput").ap()
        self.oT = nc.dram_tensor("oT", [nseq, D, SEQ], F32, kind="ExternalOutput").ap()
        self.off = 16512
        self.lim = 229344
        self._alloc_fixed()
        self.out_ops = []
        self.blk_next = 0
        self.blk_use = 0
        self.bank_rr = 0

    def sb(self, name, shape, dt, off=None):
        n = 1
        for s in shape[1:]:
            n *= s
        nbytes = n * (4 if dt == F32 else 2)
        nbytes = (nbytes + 31) // 32 * 32
        if off is None:
            off = self.off
            self.off += nbytes
            assert self.off <= self.lim, (name, self.off)
        else:
            assert off + nbytes <= self.lim, (name, off + nbytes)
        return self.nc.alloc_sbuf_tensor_at(name, list(shape), dt, offset=off), off + nbytes

    def _alloc_fixed(self):
        nc = self.nc
        self.hT, _ = self.sb("hT", [128, 8, SEQ], F32)
        self.xn, _ = self.sb("xn", [128, 8, SEQ], BF16)
        self.mix, _ = self.sb("mix", [128, 8, SEQ], BF16)
        self.wring, _ = self.sb("wring", [128, RING, 1024], BF16)
        self.cf, _ = self.sb("cfs", [128, 2176], F32)
        self.cb, _ = self.sb("cbs", [128, 1152], BF16)
        self.wmt, _ = self.sb("wmt", [128, 512], BF16)
        self.sq, _ = self.sb("sq", [128, 8, 512], BF16)
        self.ms, _ = self.sb("ms", [128, 2, 512], F32)
        self.scr0 = self.off
        self.L_h = [[LT("h%d_%d" % (c, t)) for t in range(4)] for c in range(8)]
        self.L_xn = [[LT("xn%d_%d" % (c, t)) for t in range(4)] for c in range(8)]
        self.L_mix = [[LT("mix%d_%d" % (c, t)) for t in range(4)] for c in range(8)]
        self.L_ring = [LT("ring%d" % i) for i in range(RING)]
        self.L_cf = LT("cf")
        self.L_cb = LT("cb")
        self.L_wmt = LT("wmt")
        self.L_sq = LT("sq")
        self.L_ms = [LT("ms0"), LT("ms1")]
        self.ps = [nc.alloc_psum_tensor("ps%d" % i, [128, 1024], F32) for i in range(4)]
        self.L_bank = [LT("bank%d" % i) for i in range(8)]
        o = self.scr0
        self.pbuf = []
        for i in range(4):
            t, o = self.sb("pbuf%d" % i, [128, SEQ], F32, off=o)
            self.pbuf.append(t)
        self.pb, o = self.sb("pb", [128, SEQ], BF16, off=o)
        self.L_pbuf = [[LT("pbuf%d_%d" % (i, t)) for t in range(4)] for i in range(4)]
        self.L_pb = [LT("pb_%d" % t) for t in range(4)]
        o = self.scr0
        self.rt = []
        for i in range(3):
            t, o = self.sb("rt%d" % i, [128, 512], F32, off=o)
            self.rt.append(t)
        self.L_rt = [LT("rt%d" % i) for i in range(3)]
        o = self.scr0
        self.wmv, o = self.sb("wmv", [128, 8, 512], BF16, off=o)
        self.L_wmv = LT("wmv")
        o1 = o
        self.vt = []
        self.vnb = []
        for i in range(2):
            t, o = self.sb("vt%d" % i, [128, 512], F32, off=o)
            self.vt.append(t)
        for i in range(2):
            t, o = self.sb("vnb%d" % i, [128, 512], BF16, off=o)
            self.vnb.append(t)
        self.st6, o = self.sb("st6", [128, 2, 8], F32, off=o)
        self.sgst, o = self.sb("sgst", [128, 1024], F32, off=o)
        self.L_vt = [LT("vt0"), LT("vt1")]
        self.L_vnb = [LT("vnb0"), LT("vnb1")]
        self.L_st6 = [LT("st6_0"), LT("st6_1")]
        self.L_sgst = LT("sgst")
        o = o1
        self.qT, o = self.sb("qT", [128, SEQ], BF16, off=o)
        self.kT, o = self.sb("kT", [128, SEQ], BF16, off=o)
        self.Va, o = self.sb("Va", [128, 16, 128], BF16, off=o)
        self.E = []
        self.Lp = []
        self.Cs = []
        self.Wt = []
        for l in range(2):
            t, o = self.sb("E%d" % l, [128, 2, 512], F32, off=o)
            self.E.append(t)
        for l in range(2):
            t, o = self.sb("Lp%d" % l, [128, 2, 512], BF16, off=o)
            self.Lp.append(t)
        for l in range(2):
            t, o = self.sb("Cs%d" % l, [128, 2, 512], BF16, off=o)
            self.Cs.append(t)
        for l in range(2):
            t, o = self.sb("Wt%d" % l, [128, 2, 512], BF16, off=o)
            self.Wt.append(t)
        self.L_qT = [LT("qT%d" % t) for t in range(4)]
        self.L_kT = [LT("kT%d" % t) for t in range(4)]
        self.L_Va = LT("Va")
        self.L_E = [LT("E0"), LT("E1")]
        self.L_Lp = [LT("Lp0"), LT("Lp1")]
        self.L_Cs = [LT("Cs0"), LT("Cs1")]
        self.L_Wt = [LT("Wt0"), LT("Wt1")]
        self.scr_groups = {
            "l0": [x for r in self.L_pbuf for x in r] + self.L_pb,
            "rt": self.L_rt,
            "sgu": [self.L_wmv] + self.L_vt + self.L_vnb + self.L_st6 + [self.L_sgst],
            "att": [self.L_wmv] + self.L_qT + self.L_kT + [self.L_Va] + self.L_E + self.L_Lp
                   + self.L_Cs + self.L_Wt,
        }
        self.scr_cur = None

    def use_scratch(self, grp):
        if self.scr_cur == grp:
            return
        cur = self.scr_groups[grp]
        old = []
        for k, v in self.scr_groups.items():
            if k != grp:
                old.extend(t for t in v if t not in cur)
        keep = self.scr_groups[self.scr_cur] if self.scr_cur is not None else []
        self.S.alias([t for t in cur if t not in keep], old)
        self.scr_cur = grp

    def bank(self):
        i = self.bank_rr
        self.bank_rr = (self.bank_rr + 1) % 8
        return i

    def bank_ap(self, i):
        return self.ps[i // 2][:, (i % 2) * 512:(i % 2) * 512 + 512]

    def prefetch(self):
        while self.blk_next < self.blk_use + RING and self.blk_next < self.blk_total:
            gb = self.blk_next
            slot = gb % RING
            b = gb % NBLK
            self.S.op("pool",
                      (lambda slot, b: (lambda e: e.dma_start(out=self.wring[:, slot, :], in_=self.wblk[b])))(slot, b),
                      writes=[self.L_ring[slot]], dma_key=("ring", slot))
            self.blk_next += 1

    def next_block(self):
        self.prefetch()
        gb = self.blk_use
        assert gb < self.blk_next
        self.blk_use += 1
        return gb % RING, self.L_ring[gb % RING]

    def mm(self, out_ap, pairs, reads, writes, skip=False, first=True, last=True):
        def fn(pe):
            n = len(pairs)
            ins = None
            for i, (l, r) in enumerate(pairs):
                kw = {}
                if skip:
                    kw["skip_group_check"] = True
                ins = pe.matmul(out_ap, lhsT=l, rhs=r, start=(first and i == 0),
                                stop=(last and i == n - 1), **kw)
            return ins
        return self.S.op("pe", fn, reads=reads, writes=writes)

    def proj_fm(self, rhs_buf, L_rhs, evac):
        slot, Lr = self.next_block()
        for t in range(4):
            bi = self.bank()
            pairs = [(self.wring[:, slot, k * 128:(k + 1) * 128], rhs_buf[:, k, t * 512:(t + 1) * 512])
                     for k in range(8)]
            self.mm(self.bank_ap(bi), pairs, reads=[Lr] + [L_rhs[k][t] for k in range(8)],
                    writes=[self.L_bank[bi]])
            evac(t, bi)
        self.prefetch()

    def setup(self):
        S = self.S
        S.op("sp", lambda e: e.dma_start(out=self.cf[:], in_=self.cf_d), writes=[self.L_cf], dma_key="cf")
        S.op("pool", lambda e: e.dma_start(out=self.cb[:], in_=self.cb_d), writes=[self.L_cb], dma_key="cb")
        self.use_scratch("sgu")
        S.op("sp", lambda e: e.dma_start(out=self.sgst[:], in_=self.sg_d), writes=[self.L_sgst], dma_key="sgst")
        S.op("dve", lambda e: e.tensor_tensor(out=self.wmt[:], in0=self.sgst[:, 0:512], in1=self.sgst[:, 512:1024],
                                              op=ALU.mult),
             reads=[self.L_sgst], writes=[self.L_wmt])

    def load_x(self, s):
        for c in range(8):
            self.S.op("sp", (lambda c: (lambda e: e.dma_start(out=self.hT[:, c, :],
                                                              in_=self.xT[s, c * 128:(c + 1) * 128, :])))(c),
                      writes=self.L_h[c], dma_key=("h", c))

    def rmsnorm(self, gcol, final_seq=None):
        S = self.S
        hT, sq, ms, cf = self.hT, self.sq, self.ms, self.cf
        ones = self.cb[:, 0:128]
        mhalf = cf[:, 1664:2176]
        if final_seq is not None:
            self.use_scratch("rt")
        for t in range(4):
            ts = slice(t * 512, (t + 1) * 512)
            S.op("act", (lambda ts: (lambda e: e.activation(out=sq[:, :, :], in_=hT[:, :, ts], func=AF.Square)))(ts),
                 reads=[self.L_h[c][t] for c in range(8)], writes=[self.L_sq])
            bi = self.bank()
            self.mm(self.bank_ap(bi), [(ones, sq[:, c, :]) for c in range(8)],
                    reads=[self.L_sq, self.L_cb], writes=[self.L_bank[bi]])
            m = t % 2
            S.op("dve", (lambda bi, m: (lambda e: e.tensor_scalar(out=ms[:, m, :], in0=self.bank_ap(bi),
                                                                  scalar1=1.0 / D, scalar2=EPS,
                                                                  op0=ALU.mult, op1=ALU.add)))(bi, m),
                 reads=[self.L_bank[bi]], writes=[self.L_ms[m]])
            S.op("pool", (lambda m: (lambda e: e.tensor_tensor(out=ms[:, m, :], in0=ms[:, m, :], in1=mhalf,
                                                               op=ALU.pow)))(m),
                 reads=[self.L_ms[m], self.L_cf], writes=[self.L_ms[m]])
            for c in range(8):
                g = cf[:, gcol + c:gcol + c + 1]
                if final_seq is None:
                    S.op("dve", (lambda c, ts, m, g: (lambda e: e.scalar_tensor_tensor(
                        out=self.xn[:, c, ts], in0=hT[:, c, ts], scalar=g, in1=ms[:, m, :],
                        op0=ALU.mult, op1=ALU.mult)))(c, ts, m, g),
                         reads=[self.L_h[c][t], self.L_ms[m], self.L_cf], writes=[self.L_xn[c][t]])
                else:
                    r = (t * 8 + c) % 3
                    S.op("dve", (lambda c, ts, m, g, r: (lambda e: e.scalar_tensor_tensor(
                        out=self.rt[r][:], in0=hT[:, c, ts], scalar=g, in1=ms[:, m, :],
                        op0=ALU.mult, op1=ALU.mult)))(c, ts, m, g, r),
                         reads=[self.L_h[c][t], self.L_ms[m], self.L_cf], writes=[self.L_rt[r]])
                    o = S.op("sp", (lambda c, ts, r: (lambda e: e.dma_start(
                        out=self.oT[final_seq, c * 128:(c + 1) * 128, ts], in_=self.rt[r][:])))(c, ts, r),
                             reads=[self.L_rt[r]], dma_key=("out", r))
                    self.out_ops.append(o)

    def out_proj(self):
        for c in range(8):
            def evac(t, bi, c=c):
                ts = slice(t * 512, (t + 1) * 512)
                self.S.op("dve", lambda e: e.tensor_tensor(out=self.hT[:, c, ts], in0=self.bank_ap(bi),
                                                           in1=self.hT[:, c, ts], op=ALU.add),
                          reads=[self.L_bank[bi], self.L_h[c][t]], writes=[self.L_h[c][t]])
            self.proj_fm(self.mix, self.L_mix, evac)

    def mlp(self, l):
        S = self.S
        self.rmsnorm(8 + 16 * l)
        self.use_scratch("rt")
        rr = [0]
        for q in range(4):
            for j in range(8):
                def evac(t, bi, j=j):
                    ts = slice(t * 512, (t + 1) * 512)
                    r = rr[0] % 3
                    rr[0] += 1
                    S.op("act", lambda e: e.activation(out=self.rt[r][:], in_=self.bank_ap(bi), func=AF.Relu),
                         reads=[self.L_bank[bi]], writes=[self.L_rt[r]])
                    S.op("act", lambda e: e.activation(out=self.mix[:, j, ts], in_=self.rt[r][:], func=AF.Square),
                         reads=[self.L_rt[r]], writes=[self.L_mix[j][t]])
                self.proj_fm(self.xn, self.L_xn, evac)
            for c in range(8):
                def evac2(t, bi, c=c):
                    ts = slice(t * 512, (t + 1) * 512)
                    S.op("dve", lambda e: e.tensor_tensor(out=self.hT[:, c, ts], in0=self.bank_ap(bi),
                                                          in1=self.hT[:, c, ts], op=ALU.add),
                         reads=[self.L_bank[bi], self.L_h[c][t]], writes=[self.L_h[c][t]])
                self.proj_fm(self.mix, self.L_mix, evac2)

    def layer0(self):
        S = self.S
        cf = self.cf
        self.rmsnorm(0)
        self.use_scratch("l0")
        pbuf, L_pbuf = self.pbuf, self.L_pbuf
        T = SEQ

        def evac_to(pi):
            def evac(t, bi):
                ts = slice(t * 512, (t + 1) * 512)
                S.op("act", lambda e: e.activation(out=pbuf[pi][:, ts], in_=self.bank_ap(bi), func=AF.Copy),
                     reads=[self.L_bank[bi]], writes=[L_pbuf[pi][t]])
            return evac

        def full(pi):
            return L_pbuf[pi]

        for g in range(4):
            wdw = 2 ** (g + 1)
            a, x1, x2 = g % 4, (g + 1) % 4, (g + 2) % 4
            self.proj_fm(self.xn, self.L_xn, evac_to(a))
            src = a
            d = 1
            dst_cycle = [x1, x2]
            k = 0
            while d < wdw:
                dst = dst_cycle[k % 2]
                k += 1
                S.op("dve", (lambda src, dst, d: (lambda e: e.tensor_tensor(
                    out=pbuf[dst][:, d:T], in0=pbuf[src][:, d:T], in1=pbuf[src][:, 0:T - d], op=ALU.add)))(src, dst, d),
                     reads=full(src), writes=full(dst))
                S.op("dve", (lambda src, dst, d: (lambda e: e.tensor_copy(out=pbuf[dst][:, 0:d], in_=pbuf[src][:, 0:d])))(src, dst, d),
                     reads=[L_pbuf[src][0]], writes=[L_pbuf[dst][0]])
                src = dst
                d *= 2
            S.op("dve", (lambda src, a, wdw: (lambda e: e.scalar_tensor_tensor(
                out=self.pb[:, :], in0=pbuf[src][:, :], scalar=1.0 / wdw, in1=pbuf[a][:, :],
                op0=ALU.mult, op1=ALU.subtract)))(src, a, wdw),
                 reads=full(src) + full(a), writes=self.L_pb)
            tmp = x1 if src != x1 else x2
            S.op("dve", (lambda src, tmp, g: (lambda e: e.tensor_tensor(
                out=pbuf[tmp][:, 0:16], in0=pbuf[src][:, 0:16], in1=cf[:, 64 + g * 16:64 + g * 16 + 16],
                op=ALU.mult)))(src, tmp, g),
                 reads=[L_pbuf[src][0], self.L_cf], writes=[L_pbuf[tmp][0]])
            S.op("dve", (lambda tmp, a: (lambda e: e.tensor_tensor(
                out=self.pb[:, 0:16], in0=pbuf[tmp][:, 0:16], in1=pbuf[a][:, 0:16], op=ALU.subtract)))(tmp, a),
                 reads=[L_pbuf[tmp][0], L_pbuf[a][0]], writes=[self.L_pb[0]])
            for t in range(4):
                ts = slice(t * 512, (t + 1) * 512)
                bi = self.bank()
                self.mm(self.bank_ap(bi), [(self.cb[:, 640 + g * 128:640 + (g + 1) * 128], self.pb[:, ts])],
                        reads=[self.L_cb, self.L_pb[t]], writes=[self.L_bank[bi]])
                S.op("act", (lambda ts, bi, g: (lambda e: e.activation(
                    out=self.mix[:, g, ts], in_=self.bank_ap(bi), func=AF.Copy, scale=cf[:, 40 + g:41 + g])))(ts, bi, g),
                     reads=[self.L_bank[bi], self.L_cf], writes=[self.L_mix[g][t]])
        for cc in range(4):
            b_xb, b_gc, b_gb, b_y = 0, 1, 2, 3
            self.proj_fm(self.xn, self.L_xn, evac_to(b_xb))
            self.proj_fm(self.xn, self.L_xn, evac_to(b_gc))
            S.op("dve", lambda e: e.tensor_tensor(out=pbuf[b_xb][:, :], in0=pbuf[b_xb][:, :], in1=pbuf[b_gc][:, :],
                                                  op=ALU.mult),
                 reads=full(b_xb) + full(b_gc), writes=full(b_xb))
            w0 = cf[:, 44 + 0 * 4 + cc:45 + 0 * 4 + cc]
            w1 = cf[:, 44 + 1 * 4 + cc:45 + 1 * 4 + cc]
            w2 = cf[:, 44 + 2 * 4 + cc:45 + 2 * 4 + cc]
            bb = cf[:, 56 + cc:57 + cc]
            S.op("dve", (lambda w2, bb: (lambda e: e.tensor_scalar(out=pbuf[b_y][:, :], in0=pbuf[b_xb][:, :],
                                                                  scalar1=w2, scalar2=bb, op0=ALU.mult, op1=ALU.add)))(w2, bb),
                 reads=full(b_xb) + [self.L_cf], writes=full(b_y))
            S.op("dve", (lambda w1: (lambda e: e.scalar_tensor_tensor(
                out=pbuf[b_y][:, 1:T], in0=pbuf[b_xb][:, 0:T - 1], scalar=w1, in1=pbuf[b_y][:, 1:T],
                op0=ALU.mult, op1=ALU.add)))(w1),
                 reads=full(b_xb) + full(b_y) + [self.L_cf], writes=full(b_y))
            S.op("dve", (lambda w0: (lambda e: e.scalar_tensor_tensor(
                out=pbuf[b_y][:, 2:T], in0=pbuf[b_xb][:, 0:T - 2], scalar=w0, in1=pbuf[b_y][:, 2:T],
                op0=ALU.mult, op1=ALU.add)))(w0),
                 reads=full(b_xb) + full(b_y) + [self.L_cf], writes=full(b_y))
            self.proj_fm(self.xn, self.L_xn, evac_to(b_gb))
            for t in range(4):
                ts = slice(t * 512, (t + 1) * 512)
                S.op("dve", (lambda ts, cc: (lambda e: e.tensor_tensor(
                    out=self.mix[:, 4 + cc, ts], in0=pbuf[b_y][:, ts], in1=pbuf[b_gb][:, ts], op=ALU.mult)))(ts, cc),
                     reads=[L_pbuf[b_y][t], L_pbuf[b_gb][t]], writes=[self.L_mix[4 + cc][t]])
        self.out_proj()
        self.mlp(0)

    def layer1(self):
        S = self.S
        cf = self.cf
        self.rmsnorm(16)
        self.use_scratch("sgu")
        S.op("pool", lambda e: e.dma_start(out=self.wmv[:, :, :], in_=self.wmov[0].rearrange("p (k n) -> p k n", k=8)),
             writes=[self.L_wmv], dma_key="wmv")
        for g in range(4):
            def evac(t, bi, g=g):
                ts = slice(t * 512, (t + 1) * 512)
                S.op("act", lambda e: e.activation(out=self.mix[:, g, ts], in_=self.bank_ap(bi), func=AF.Gelu),
                     reads=[self.L_bank[bi]], writes=[self.L_mix[g][t]])
            self.proj_fm(self.xn, self.L_xn, evac)
        gbc = cf[:, 128:640]
        bbc = cf[:, 640:1152]
        bsb = cf[:, 1152:1664]
        for n in range(16):
            t = n // 4
            ns = slice(n * 128, (n + 1) * 128)
            i = n % 2
            bi = self.bank()
            pairs = [(self.xn[:, k, ns], self.wmv[:, k, :]) for k in range(8)]
            self.mm(self.bank_ap(bi), pairs, reads=[self.L_wmv] + [self.L_xn[k][t] for k in range(8)],
                    writes=[self.L_bank[bi]])
            vt, vnb, st6 = self.vt[i], self.vnb[i], self.st6
            S.op("act", (lambda bi, vt: (lambda e: e.activation(out=vt[:], in_=self.bank_ap(bi), func=AF.Gelu)))(bi, vt),
                 reads=[self.L_bank[bi]], writes=[self.L_vt[i]])
            S.op("dve", (lambda vt, i: (lambda e: e.bn_stats(out=st6[:, i, 0:6], in_=vt[:])))(vt, i),
                 reads=[self.L_vt[i]], writes=[self.L_st6[i]])
            S.op("dve", (lambda i: (lambda e: e.bn_aggr(out=st6[:, i, 6:8], in_=st6[:, i, 0:6])))(i),
                 reads=[self.L_st6[i]], writes=[self.L_st6[i]])
            S.op("dve", (lambda i: (lambda e: e.tensor_scalar(out=st6[:, i, 7:8], in0=st6[:, i, 7:8], scalar1=EPS,
                                                             scalar2=None, op0=ALU.add)))(i),
                 reads=[self.L_st6[i]], writes=[self.L_st6[i]])
            S.op("pool", (lambda i: (lambda e: e.tensor_tensor(out=st6[:, i, 7:8], in0=st6[:, i, 7:8],
                                                              in1=cf[:, 1664:1665], op=ALU.pow)))(i),
                 reads=[self.L_st6[i], self.L_cf], writes=[self.L_st6[i]])
            S.op("dve", (lambda vt, i: (lambda e: e.tensor_scalar(out=vt[:], in0=vt[:], scalar1=st6[:, i, 6:7],
                                                                 scalar2=st6[:, i, 7:8], op0=ALU.subtract,
                                                                 op1=ALU.mult)))(vt, i),
                 reads=[self.L_vt[i], self.L_st6[i]], writes=[self.L_vt[i]])
            S.op("dve", (lambda vt: (lambda e: e.tensor_tensor(out=vt[:], in0=vt[:], in1=gbc, op=ALU.mult)))(vt),
                 reads=[self.L_vt[i], self.L_cf], writes=[self.L_vt[i]])
            S.op("dve", (lambda vt, vnb: (lambda e: e.tensor_tensor(out=vnb[:], in0=vt[:], in1=bbc, op=ALU.add)))(vt, vnb),
                 reads=[self.L_vt[i], self.L_cf], writes=[self.L_vnb[i]])
            bj = self.bank()

            def fn(pe, bj=bj, vnb=vnb):
                ins = None
                for g in range(4):
                    ins = pe.matmul(self.bank_ap(bj)[:, g * 128:(g + 1) * 128], lhsT=vnb[:, g * 128:(g + 1) * 128],
                                    rhs=self.wmt[:, g * 128:(g + 1) * 128], start=True, stop=True,
                                    skip_group_check=True)
                return ins
            S.op("pe", fn, reads=[self.L_vnb[i], self.L_wmt], writes=[self.L_bank[bj]])
            S.op("dve", (lambda bj, vt: (lambda e: e.tensor_tensor(out=vt[:], in0=self.bank_ap(bj), in1=bsb,
                                                                  op=ALU.add)))(bj, vt),
                 reads=[self.L_bank[bj], self.L_cf], writes=[self.L_vt[i]])
            S.op("dve", (lambda vt, ns: (lambda e: e.tensor_tensor(
                out=self.mix[:, 0:4, ns], in0=vt[:].rearrange("p (g t) -> p g t", g=4), in1=self.mix[:, 0:4, ns],
                op=ALU.mult)))(vt, ns),
                 reads=[self.L_vt[i]] + [self.L_mix[g][t] for g in range(4)],
                 writes=[self.L_mix[g][t] for g in range(4)])
        self.use_scratch("att")
        S.op("pool", lambda e: e.dma_start(out=self.wmv[:, :, :], in_=self.wmov[1].rearrange("p (k n) -> p k n", k=8)),
             writes=[self.L_wmv], dma_key="wmv")
        ident = self.cb[:, 128:256]
        nui = self.cb[:, 256:384]
        nones = self.cb[:, 384:512]
        negm = self.cb[:, 512:640]
        ZB = [0, 1]
        AVB = [4, 5]
        self.bank_rr = 6

        def gbank():
            i = self.bank_rr
            self.bank_rr = 6 + (self.bank_rr - 6 + 1) % 2
            return i

        for a in range(4):
            for which, dst, Ld in ((0, self.qT, self.L_qT), (1, self.kT, self.L_kT)):
                slot, Lr = self.next_block()
                for t in range(4):
                    ts = slice(t * 512, (t + 1) * 512)
                    bi = gbank()
                    pairs = [(self.wring[:, slot, k * 128:(k + 1) * 128], self.xn[:, k, ts]) for k in range(8)]
                    self.mm(self.bank_ap(bi), pairs, reads=[Lr] + [self.L_xn[k][t] for k in range(8)],
                            writes=[self.L_bank[bi]])
                    sc = 0.125 if which == 0 else 1.0
                    S.op("dve", (lambda dst, ts, bi, sc: (lambda e: e.tensor_scalar(
                        out=dst[:, ts], in0=self.bank_ap(bi), scalar1=sc, scalar2=None, op0=ALU.mult)))(dst, ts, bi, sc),
                         reads=[self.L_bank[bi]], writes=[Ld[t]])
                self.prefetch()
            for n4 in range(4):
                bi = gbank()

                def fnv(pe, bi=bi, n4=n4, a=a):
                    ins = None
                    for nn in range(4):
                        n = n4 * 4 + nn
                        for k in range(8):
                            ins = pe.matmul(self.bank_ap(bi)[:, nn * 128:(nn + 1) * 128],
                                            lhsT=self.xn[:, k, n * 128:(n + 1) * 128],
                                            rhs=self.wmv[:, k, a * 128:(a + 1) * 128],
                                            start=(k == 0), stop=(k == 7), skip_group_check=True)
                    return ins
                S.op("pe", fnv, reads=[self.L_wmv] + [self.L_xn[k][n4] for k in range(8)], writes=[self.L_bank[bi]])
                S.op("dve", (lambda bi, n4: (lambda e: e.tensor_copy(
                    out=self.Va[:, n4 * 4:(n4 + 1) * 4, :],
                    in_=self.bank_ap(bi).rearrange("p (n d) -> p n d", n=4))))(bi, n4),
                     reads=[self.L_bank[bi]], writes=[self.L_Va])
            lanes = [[(3, b) for b in range(15, -1, -1)] + [(0, b) for b in range(3, -1, -1)],
                     [(2, b) for b in range(11, -1, -1)] + [(1, b) for b in range(7, -1, -1)]]
            for si in range(20):
                for ln in range(2):
                    Tq, b = lanes[ln][si]
                    self.att_step(a, ln, Tq, b, ZB[ln], AVB[ln], ident, nui, nones, negm)
        self.bank_rr = 0
        self.out_proj()
        self.mlp(1)

    def att_step(self, a, ln, Tq, b, zi, avb, ident, nui, nones, negm):
        S = self.S
        first = (b == 4 * Tq + 3)
        diag = (b >= 4 * Tq)
        c0 = max(0, b - 4 * Tq) * 128
        q0 = Tq * 512 + c0
        q1 = (Tq + 1) * 512
        zps = self.ps[zi]
        L_z = [self.L_bank[2 * zi], self.L_bank[2 * zi + 1]]
        E, Lp, Cs, Wt = self.E[ln], self.Lp[ln], self.Cs[ln], self.Wt[ln]
        L_E, L_Lp, L_Cs, L_Wt = self.L_E[ln], self.L_Lp[ln], self.L_Cs[ln], self.L_Wt[ln]
        ks = slice(b * 128, (b + 1) * 128)
        L_q = [self.L_qT[Tq]]
        L_k = [self.L_kT[b // 4]]

        def zmm(pe, with_t):
            ins = None
            for h in range(2):
                hp = slice(h * 64, (h + 1) * 64)
                out = zps[:, h * 512 + c0:h * 512 + 512]
                seq = [(self.kT[hp, ks], self.qT[hp, q0:q1])]
                if with_t:
                    seq.append((nui, Lp[:, h, c0:512]))
                    if not first:
                        seq.append((nones, Cs[:, h, c0:512]))
                n = len(seq) + (1 if diag else 0)
                for i, (l, r) in enumerate(seq):
                    ins = pe.matmul(out, lhsT=l, rhs=r, start=(i == 0), stop=(i == n - 1), skip_group_check=True)
                if diag:
                    ins = pe.matmul(zps[:, h * 512 + c0:h * 512 + c0 + 128], lhsT=ident, rhs=negm,
                                    start=False, stop=True, skip_group_check=True)
            return ins

        S.op("pe", lambda pe: zmm(pe, False), reads=L_q + L_k + [self.L_cb], writes=L_z)
        zv = zps[:].rearrange("p (h n) -> p h n", h=2)[:, :, c0:512]
        S.op("act", lambda e: e.activation(out=E[:, :, c0:512], in_=zv, func=AF.Exp), reads=L_z, writes=[L_E])
        S.op("act", lambda e: e.activation(out=Lp[:, :, c0:512], in_=E[:, :, c0:512], func=AF.Ln, bias=1.0),
             reads=[L_E], writes=[L_Lp])
        rd = L_q + L_k + [self.L_cb, L_Lp] + ([] if first else [L_Cs])
        S.op("pe", lambda pe: zmm(pe, True), reads=rd, writes=L_z)
        if first:
            S.op("pool", lambda e: e.memset(Cs[:, :, :], 0.0), writes=[L_Cs])
        if b > 0:
            S.op("pool", lambda e: e.tensor_tensor(out=Cs[:, :, c0:512], in0=Cs[:, :, c0:512], in1=Lp[:, :, c0:512],
                                                   op=ALU.add),
                 reads=[L_Cs, L_Lp], writes=[L_Cs])
        S.op("act", lambda e: e.activation(out=Wt[:, :, c0:512], in_=zv, func=AF.Exp), reads=L_z, writes=[L_Wt])
        avp = self.bank_ap(avb)

        def avmm(pe):
            ins = None
            for h in range(2):
                ins = pe.matmul(avp[h * 64:(h + 1) * 64, c0:512], lhsT=self.Va[:, b, h * 64:(h + 1) * 64],
                                rhs=Wt[:, h, c0:512], start=first, stop=(b == 0), skip_group_check=True)
            return ins
        S.op("pe", avmm, reads=[self.L_Va, L_Wt], writes=[self.L_bank[avb]])
        if b == 0:
            ts = slice(Tq * 512, (Tq + 1) * 512)
            S.op("dve", lambda e: e.tensor_copy(out=self.mix[:, 4 + a, ts], in_=avp),
                 reads=[self.L_bank[avb]], writes=[self.L_mix[4 + a][Tq]])

    def build(self):
        per_seq = 0
        if 0 in self.layers:
            per_seq += 88
        if 1 in self.layers:
            per_seq += 84
        self.blk_total = None
        self.blk_total = self.nseq * NBLK
        self.setup()
        for s in range(self.nseq):
            assert self.blk_use == s * NBLK
            self.load_x(s)
            self.layer0()
            self.layer1()
            self.rmsnorm(32, final_seq=s)
        self.S.emit(final_waits=self.out_ops)
        return self.nc


_CACHE = {}


def kernel(**inputs):
    x = np.asarray(inputs["x"], np.float32)
    wts = _prep_weights({k: np.asarray(v, np.float32) for k, v in inputs.items() if k != "x"})
    if "nc" not in _CACHE:
        _CACHE["nc"] = Builder().build()
    nc = _CACHE["nc"]
    in_maps = []
    for c in range(NCORES):
        xs = np.ascontiguousarray(x[c * NSEQ:(c + 1) * NSEQ].transpose(0, 2, 1))
        m = {"xT": xs}
        m.update(wts)
        in_maps.append(m)
    res = run_bass_kernel_spmd(nc, in_maps, core_ids=list(range(NCORES)))
    out = np.empty((NCORES * NSEQ, SEQ, D), np.float32)
    for c in range(NCORES):
        out[c * NSEQ:(c + 1) * NSEQ] = res.results[c]["oT"].transpose(0, 2, 1)
    return out
```

```python
import contextlib
import numpy as np
import concourse.bass as bass
import concourse.mybir as mybir
from concourse.bass_utils import run_bass_kernel_spmd

F32 = mybir.dt.float32
BF16 = mybir.dt.bfloat16
AF = mybir.ActivationFunctionType
ALU = mybir.AluOpType

NCORES = 8
SEQ = 2048
D = 1024
NSEQ = 2
EPS = 1e-6
RING = 5
NEG = -30000.0

ENGS = ("pe", "act", "dve", "pool", "sp")


class LT:
    __slots__ = ("name", "w", "rs")

    def __init__(self, name):
        self.name = name
        self.w = None
        self.rs = []


class Op:
    __slots__ = ("eng", "fn", "waits", "dma", "chan", "val", "sig", "vc", "idx", "val2")


class Sched:
    def __init__(self, nc):
        self.nc = nc
        self.ops = {e: [] for e in ENGS}
        self.cur = {e: {} for e in ENGS}
        self.dma_gen = {}

    def alias(self, new_lts, old_lts):
        pend = []
        for t in old_lts:
            if t.w is not None:
                pend.append(t.w)
            pend.extend(t.rs)
        for t in new_lts:
            t.rs = list(t.rs) + pend

    def op(self, eng, fn, reads=(), writes=(), dma_key=None):
        o = Op()
        o.eng = eng
        o.fn = fn
        o.dma = dma_key is not None
        o.sig = o.dma
        o.idx = len(self.ops[eng])
        deps = []
        for t in reads:
            if t.w is not None:
                deps.append(t.w)
        for t in writes:
            if t.w is not None:
                deps.append(t.w)
            deps.extend(t.rs)
        cur = self.cur[eng]
        waits = []
        for d in deps:
            if (not d.dma) and (not o.dma) and d.eng == "pe" and eng == "pe":
                continue
            if cur.get(d.chan, -1) >= d.val:
                continue
            waits.append(d)
            d.sig = True
            for k, v in d.vc.items():
                if cur.get(k, -1) < v:
                    cur[k] = v
        best = {}
        for d in waits:
            if d.chan not in best or best[d.chan].val < d.val:
                best[d.chan] = d
        o.waits = list(best.values())
        if o.dma:
            g = self.dma_gen.get(dma_key, 0) + 1
            self.dma_gen[dma_key] = g
            o.chan = ("dma", dma_key)
            o.val = g
        else:
            o.chan = eng
            o.val = o.idx
        vc = dict(cur)
        vc[o.chan] = o.val
        o.vc = vc
        for t in reads:
            t.rs.append(o)
        for t in writes:
            t.w = o
            t.rs = []
        self.ops[eng].append(o)
        return o

    def emit(self, final_waits=()):
        nc = self.nc
        for e in ENGS:
            c = 0
            for o in self.ops[e]:
                if o.dma:
                    continue
                if o.sig:
                    c += 1
                    o.val2 = c
        dma_keys = list(self.dma_gen.keys())
        with contextlib.ExitStack() as st:
            esem = {e: st.enter_context(nc.semaphore("s_" + e)) for e in ENGS}
            dsem = {k: st.enter_context(nc.semaphore("d_%d" % i)) for i, k in enumerate(dma_keys)}
            block = st.enter_context(nc.Block())

            def semval(d):
                if d.dma:
                    return dsem[d.chan[1]], 16 * d.val
                return esem[d.eng], d.val2

            def run(engname):
                def body(eng):
                    for o in self.ops[engname]:
                        for d in o.waits:
                            s, v = semval(d)
                            eng.wait_ge(s, v)
                        ins = o.fn(eng)
                        if o.sig:
                            if o.dma:
                                ins.then_inc(dsem[o.chan[1]], 16)
                            else:
                                ins.then_inc(esem[engname], 1)
                    if engname == "sp":
                        for d in final_waits:
                            s, v = semval(d)
                            eng.wait_ge(s, v)
                return body

            block.tensor(run("pe"))
            block.scalar(run("act"))
            block.vector(run("dve"))
            block.gpsimd(run("pool"))
            block.sync(run("sp"))


def _blk(W, col0):
    return np.ascontiguousarray(
        W[:, col0:col0 + 128].reshape(8, 128, 128).transpose(1, 0, 2)).reshape(128, 1024)


def _mov(W, col0):
    return np.ascontiguousarray(
        W[:, col0:col0 + 512].reshape(8, 128, 512).transpose(1, 0, 2)).reshape(128, 4096)


L0_OC_ORDER = [0, 1, 2, 3] + [o for cc in range(4) for o in (4 + cc, 12 + cc, 8 + cc)]


def _prep_weights(inp):
    blocks = []
    w = inp["ab_w_in"][0]
    for oc in L0_OC_ORDER:
        blocks.append(_blk(w, oc * 128))
    w = inp["ab_w_out"][0]
    for c in range(8):
        blocks.append(_blk(w, c * 128))

    def mlp_blocks(l):
        w1 = inp["mlp_w1"][l]
        w2 = inp["mlp_w2"][l]
        for q in range(4):
            for j in range(8):
                blocks.append(_blk(w1, (q * 8 + j) * 128))
            for c in range(8):
                blocks.append(_blk(w2[q * 1024:(q + 1) * 1024], c * 128))

    mlp_blocks(0)
    w = inp["cd_w_in"][0]
    for g in range(4):
        blocks.append(_blk(w, g * 128))
    for a in range(4):
        blocks.append(_blk(w, 1024 + a * 128))
        blocks.append(_blk(w, 1536 + a * 128))
    wo = inp["cd_w_out"][0]
    for c in range(8):
        blocks.append(_blk(wo, c * 128))
    mlp_blocks(1)
    wblk = np.ascontiguousarray(np.stack(blocks, 0)).astype(np.float32, copy=False)
    wmov = np.ascontiguousarray(np.stack([_mov(w, 512), _mov(w, 2048)], 0))

    def col8(v):
        return v.reshape(8, 128).T

    gv = np.zeros((128, 64), np.float32)
    gv[:, 0:8] = col8(inp["mix_norm_g"][0])
    gv[:, 8:16] = col8(inp["mlp_norm_g"][0])
    gv[:, 16:24] = col8(inp["mix_norm_g"][1])
    gv[:, 24:32] = col8(inp["mlp_norm_g"][1])
    gv[:, 32:40] = col8(inp["final_norm_g"])
    gv[:, 40:44] = inp["pool_scale"][0].T
    gv[:, 44:56] = inp["conv_w"][0].reshape(3, 4, 128).transpose(2, 0, 1).reshape(128, 12)
    gv[:, 56:60] = inp["conv_b"][0].reshape(4, 128).T
    gv[:, 60] = EPS
    rc = np.zeros((128, 4, 16), np.float32)
    for g, wdw in enumerate((2, 4, 8, 16)):
        rc[:, g, :] = 1.0 / np.minimum(np.arange(16) + 1, wdw)
    cf = np.concatenate([
        gv, rc.reshape(128, 64),
        np.tile(inp["sgu_norm_g"][0].reshape(1, 512), (128, 1)),
        np.tile(inp["sgu_norm_b"][0].reshape(1, 512), (128, 1)),
        np.tile(inp["sgu_b"][0].reshape(1, 512), (128, 1)),
        np.full((128, 512), -0.5, np.float32),
    ], axis=1).astype(np.float32)
    j = np.arange(128)[:, None]
    k = np.arange(128)[None, :]
    ones = np.ones((128, 128), np.float32)
    ident = (j == k).astype(np.float32)
    nui = -(j >= k).astype(np.float32)
    nones = -ones
    negm = np.where(j >= k, NEG, 0.0).astype(np.float32)
    poolw = inp["pool_w"][0].transpose(1, 0, 2).reshape(128, 512)
    cb = np.concatenate([ones, ident, nui, nones, negm, poolw], axis=1).astype(np.float32)
    wT = inp["sgu_w"][0].transpose(2, 0, 1).reshape(128, 512)
    tril = np.tile((j <= k).astype(np.float32), (1, 4))
    sg = np.concatenate([wT, tril], axis=1).astype(np.float32)
    return dict(wblk=wblk, wmov=wmov, cf=cf, cb=cb, sg=sg)


NBLK = 172


class Builder:
    def __init__(self, nseq=NSEQ, layers=(0, 1), debug=False):
        self.nseq = nseq
        self.layers = layers
        nc = self.nc = bass.Bass("TRN2", target_bir_lowering=False)
        self.S = Sched(nc)
        self.xT = nc.dram_tensor("xT", [nseq, D, SEQ], F32, kind="ExternalInput").ap()
        self.wblk = nc.dram_tensor("wblk", [NBLK, 128, 1024], F32, kind="ExternalInput").ap()
        self.wmov = nc.dram_tensor("wmov", [2, 128, 4096], F32, kind="ExternalInput").ap()
        self.cf_d = nc.dram_tensor("cf", [128, 2176], F32, kind="ExternalInput").ap()
        self.cb_d = nc.dram_tensor("cb", [128, 1152], F32, kind="ExternalInput").ap()
        self.sg_d = nc.dram_tensor("sg", [128, 1024], F32, kind="ExternalInput").ap()
        self.oT = nc.dram_tensor("oT", [nseq, D, SEQ], F32, kind="ExternalOutput").ap()
        self.off = 16512
        self.lim = 229344
        self._alloc_fixed()
        self.out_ops = []
        self.blk_next = 0
        self.blk_use = 0
        self.bank_rr = 0

    def sb(self, name, shape, dt, off=None):
        n = 1
        for s in shape[1:]:
            n *= s
        nbytes = n * (4 if dt == F32 else 2)
        nbytes = (nbytes + 31) // 32 * 32
        if off is None:
            off = self.off
            self.off += nbytes
            assert self.off <= self.lim, (name, self.off)
        else:
            assert off + nbytes <= self.lim, (name, off + nbytes)
        return self.nc.alloc_sbuf_tensor_at(name, list(shape), dt, offset=off), off + nbytes

    def _alloc_fixed(self):
        nc = self.nc
        self.hT, _ = self.sb("hT", [128, 8, SEQ], F32)
        self.xn, _ = self.sb("xn", [128, 8, SEQ], BF16)
        self.mix, _ = self.sb("mix", [128, 8, SEQ], BF16)
        self.wring, _ = self.sb("wring", [128, RING, 1024], BF16)
        self.cf, _ = self.sb("cfs", [128, 2176], F32)
        self.cb, _ = self.sb("cbs", [128, 1152], BF16)
        self.wmt, _ = self.sb("wmt", [128, 512], BF16)
        self.sq, _ = self.sb("sq", [128, 8, 512], BF16)
        self.ms, _ = self.sb("ms", [128, 2, 512], F32)
        self.scr0 = self.off
        self.L_h = [[LT("h%d_%d" % (c, t)) for t in range(4)] for c in range(8)]
        self.L_xn = [[LT("xn%d_%d" % (c, t)) for t in range(4)] for c in range(8)]
        self.L_mix = [[LT("mix%d_%d" % (c, t)) for t in range(4)] for c in range(8)]
        self.L_ring = [LT("ring%d" % i) for i in range(RING)]
        self.L_cf = LT("cf")
        self.L_cb = LT("cb")
        self.L_wmt = LT("wmt")
        self.L_sq = LT("sq")
        self.L_ms = [LT("ms0"), LT("ms1")]
        self.ps = [nc.alloc_psum_tensor("ps%d" % i, [128, 1024], F32) for i in range(4)]
        self.L_bank = [LT("bank%d" % i) for i in range(8)]
        o = self.scr0
        self.pbuf = []
        for i in range(4):
            t, o = self.sb("pbuf%d" % i, [128, SEQ], F32, off=o)
            self.pbuf.append(t)
        self.pb, o = self.sb("pb", [128, SEQ], BF16, off=o)
        self.L_pbuf = [[LT("pbuf%d_%d" % (i, t)) for t in range(4)] for i in range(4)]
        self.L_pb = [LT("pb_%d" % t) for t in range(4)]
        o = self.scr0
        self.rt = []
        for i in range(3):
            t, o = self.sb("rt%d" % i, [128, 512], F32, off=o)
            self.rt.append(t)
        self.L_rt = [LT("rt%d" % i) for i in range(3)]
        o = self.scr0
        self.wmv, o = self.sb("wmv", [128, 8, 512], BF16, off=o)
        self.L_wmv = LT("wmv")
        o1 = o
        self.vt = []
        self.vnb = []
        for i in range(2):
            t, o = self.sb("vt%d" % i, [128, 512], F32, off=o)
            self.vt.append(t)
        for i in range(2):
            t, o = self.sb("vnb%d" % i, [128, 512], BF16, off=o)
            self.vnb.append(t)
        self.st6, o = self.sb("st6", [128, 2, 8], F32, off=o)
        self.sgst, o = self.sb("sgst", [128, 1024], F32, off=o)
        self.L_vt = [LT("vt0"), LT("vt1")]
        self.L_vnb = [LT("vnb0"), LT("vnb1")]
        self.L_st6 = [LT("st6_0"), LT("st6_1")]
        self.L_sgst = LT("sgst")
        o = o1
        self.qT, o = self.sb("qT", [128, SEQ], BF16, off=o)
        self.kT, o = self.sb("kT", [128, SEQ], BF16, off=o)
        self.Va, o = self.sb("Va", [128, 16, 128], BF16, off=o)
        self.E = []
        self.Lp = []
        self.Cs = []
        self.Wt = []
        for l in range(2):
            t, o = self.sb("E%d" % l, [128, 2, 512], F32, off=o)
            self.E.append(t)
        for l in range(2):
            t, o = self.sb("Lp%d" % l, [128, 2, 512], BF16, off=o)
            self.Lp.append(t)
        for l in range(2):
            t, o = self.sb("Cs%d" % l, [128, 2, 512], BF16, off=o)
            self.Cs.append(t)
        for l in range(2):
            t, o = self.sb("Wt%d" % l, [128, 2, 512], BF16, off=o)
            self.Wt.append(t)
        self.L_qT = [LT("qT%d" % t) for t in range(4)]
        self.L_kT = [LT("kT%d" % t) for t in range(4)]
        self.L_Va = LT("Va")
        self.L_E = [LT("E0"), LT("E1")]
        self.L_Lp = [LT("Lp0"), LT("Lp1")]
        self.L_Cs = [LT("Cs0"), LT("Cs1")]
        self.L_Wt = [LT("Wt0"), LT("Wt1")]
        self.L_E4 = [LT("E4_%d" % i) for i in range(4)]
        self.L_Lp4 = [LT("Lp4_%d" % i) for i in range(4)]
        self.L_Cs4 = [LT("Cs4_%d" % i) for i in range(4)]
        self.L_Wt4 = [LT("Wt4_%d" % i) for i in range(4)]
        self.L_av = [LT("av_%d" % i) for i in range(4)]
        self.scr_groups = {
            "l0": [x for r in self.L_pbuf for x in r] + self.L_pb,
            "rt": self.L_rt,
            "sgu": [self.L_wmv] + self.L_vt + self.L_vnb + self.L_st6 + [self.L_sgst],
            "att": [self.L_wmv] + self.L_qT + self.L_kT + [self.L_Va] + self.L_E4 + self.L_Lp4
                   + self.L_Cs4 + self.L_Wt4,
        }
        self.scr_cur = None

    def use_scratch(self, grp):
        if self.scr_cur == grp:
            return
        cur = self.scr_groups[grp]
        old = []
        for k, v in self.scr_groups.items():
            if k != grp:
                old.extend(t for t in v if t not in cur)
        keep = self.scr_groups[self.scr_cur] if self.scr_cur is not None else []
        self.S.alias([t for t in cur if t not in keep], old)
        self.scr_cur = grp

    def bank(self):
        i = self.bank_rr
        self.bank_rr = (self.bank_rr + 1) % 8
        return i

    def bank_ap(self, i):
        return self.ps[i // 2][:, (i % 2) * 512:(i % 2) * 512 + 512]

    def prefetch(self):
        while self.blk_next < self.blk_use + RING and self.blk_next < self.blk_total:
            gb = self.blk_next
            slot = gb % RING
            b = gb % NBLK
            self.S.op("pool",
                      (lambda slot, b: (lambda e: e.dma_start(out=self.wring[:, slot, :], in_=self.wblk[b])))(slot, b),
                      writes=[self.L_ring[slot]], dma_key=("ring", slot))
            self.blk_next += 1

    def next_block(self):
        self.prefetch()
        gb = self.blk_use
        assert gb < self.blk_next
        self.blk_use += 1
        return gb % RING, self.L_ring[gb % RING]

    def mm(self, out_ap, pairs, reads, writes, skip=False, first=True, last=True):
        def fn(pe):
            n = len(pairs)
            ins = None
            for i, (l, r) in enumerate(pairs):
                kw = {}
                if skip:
                    kw["skip_group_check"] = True
                ins = pe.matmul(out_ap, lhsT=l, rhs=r, start=(first and i == 0),
                                stop=(last and i == n - 1), **kw)
            return ins
        return self.S.op("pe", fn, reads=reads, writes=writes)

    def proj_fm(self, rhs_buf, L_rhs, evac):
        slot, Lr = self.next_block()
        for t in range(4):
            bi = self.bank()
            pairs = [(self.wring[:, slot, k * 128:(k + 1) * 128], rhs_buf[:, k, t * 512:(t + 1) * 512])
                     for k in range(8)]
            self.mm(self.bank_ap(bi), pairs, reads=[Lr] + [L_rhs[k][t] for k in range(8)],
                    writes=[self.L_bank[bi]])
            evac(t, bi)
        self.prefetch()

    def setup(self):
        S = self.S
        S.op("sp", lambda e: e.dma_start(out=self.cf[:], in_=self.cf_d), writes=[self.L_cf], dma_key="cf")
        S.op("pool", lambda e: e.dma_start(out=self.cb[:], in_=self.cb_d), writes=[self.L_cb], dma_key="cb")
        self.use_scratch("sgu")
        S.op("sp", lambda e: e.dma_start(out=self.sgst[:], in_=self.sg_d), writes=[self.L_sgst], dma_key="sgst")
        S.op("dve", lambda e: e.tensor_tensor(out=self.wmt[:], in0=self.sgst[:, 0:512], in1=self.sgst[:, 512:1024],
                                              op=ALU.mult),
             reads=[self.L_sgst], writes=[self.L_wmt])

    def load_x(self, s):
        for c in range(8):
            self.S.op("sp", (lambda c: (lambda e: e.dma_start(out=self.hT[:, c, :],
                                                              in_=self.xT[s, c * 128:(c + 1) * 128, :])))(c),
                      writes=self.L_h[c], dma_key=("h", c))

    def rmsnorm(self, gcol, final_seq=None):
        S = self.S
        hT, sq, ms, cf = self.hT, self.sq, self.ms, self.cf
        ones = self.cb[:, 0:128]
        mhalf = cf[:, 1664:2176]
        if final_seq is not None:
            self.use_scratch("rt")
        for t in range(4):
            ts = slice(t * 512, (t + 1) * 512)
            S.op("act", (lambda ts: (lambda e: e.activation(out=sq[:, :, :], in_=hT[:, :, ts], func=AF.Square)))(ts),
                 reads=[self.L_h[c][t] for c in range(8)], writes=[self.L_sq])
            bi = self.bank()
            self.mm(self.bank_ap(bi), [(ones, sq[:, c, :]) for c in range(8)],
                    reads=[self.L_sq, self.L_cb], writes=[self.L_bank[bi]])
            m = t % 2
            S.op("act", (lambda bi, m: (lambda e: e.activation(out=ms[:, m, :], in_=self.bank_ap(bi), func=AF.Ln,
                                                               scale=1.0 / D, bias=cf[:, 60:61])))(bi, m),
                 reads=[self.L_bank[bi], self.L_cf], writes=[self.L_ms[m]])
            S.op("act", (lambda m: (lambda e: e.activation(out=ms[:, m, :], in_=ms[:, m, :], func=AF.Exp,
                                                           scale=-0.5)))(m),
                 reads=[self.L_ms[m]], writes=[self.L_ms[m]])
            for c in range(8):
                g = cf[:, gcol + c:gcol + c + 1]
                if final_seq is None:
                    S.op("dve", (lambda c, ts, m, g: (lambda e: e.scalar_tensor_tensor(
                        out=self.xn[:, c, ts], in0=hT[:, c, ts], scalar=g, in1=ms[:, m, :],
                        op0=ALU.mult, op1=ALU.mult)))(c, ts, m, g),
                         reads=[self.L_h[c][t], self.L_ms[m], self.L_cf], writes=[self.L_xn[c][t]])
                else:
                    r = (t * 8 + c) % 3
                    S.op("dve", (lambda c, ts, m, g, r: (lambda e: e.scalar_tensor_tensor(
                        out=self.rt[r][:], in0=hT[:, c, ts], scalar=g, in1=ms[:, m, :],
                        op0=ALU.mult, op1=ALU.mult)))(c, ts, m, g, r),
                         reads=[self.L_h[c][t], self.L_ms[m], self.L_cf], writes=[self.L_rt[r]])
                    o = S.op("sp", (lambda c, ts, r: (lambda e: e.dma_start(
                        out=self.oT[final_seq, c * 128:(c + 1) * 128, ts], in_=self.rt[r][:])))(c, ts, r),
                             reads=[self.L_rt[r]], dma_key=("out", r))
                    self.out_ops.append(o)

    def out_proj(self):
        for c in range(8):
            def evac(t, bi, c=c):
                ts = slice(t * 512, (t + 1) * 512)
                self.S.op("dve", lambda e: e.tensor_tensor(out=self.hT[:, c, ts], in0=self.bank_ap(bi),
                                                           in1=self.hT[:, c, ts], op=ALU.add),
                          reads=[self.L_bank[bi], self.L_h[c][t]], writes=[self.L_h[c][t]])
            self.proj_fm(self.mix, self.L_mix, evac)

    def mlp(self, l):
        S = self.S
        self.rmsnorm(8 + 16 * l)
        self.use_scratch("rt")
        rr = [0]
        for q in range(4):
            for j in range(8):
                def evac(t, bi, j=j):
                    ts = slice(t * 512, (t + 1) * 512)
                    r = rr[0] % 3
                    rr[0] += 1
                    S.op("act", lambda e: e.activation(out=self.rt[r][:], in_=self.bank_ap(bi), func=AF.Relu),
                         reads=[self.L_bank[bi]], writes=[self.L_rt[r]])
                    S.op("act", lambda e: e.activation(out=self.mix[:, j, ts], in_=self.rt[r][:], func=AF.Square),
                         reads=[self.L_rt[r]], writes=[self.L_mix[j][t]])
                self.proj_fm(self.xn, self.L_xn, evac)
            for c in range(8):
                def evac2(t, bi, c=c):
                    ts = slice(t * 512, (t + 1) * 512)
                    S.op("dve", lambda e: e.tensor_tensor(out=self.hT[:, c, ts], in0=self.bank_ap(bi),
                                                          in1=self.hT[:, c, ts], op=ALU.add),
                         reads=[self.L_bank[bi], self.L_h[c][t]], writes=[self.L_h[c][t]])
                self.proj_fm(self.mix, self.L_mix, evac2)

    def layer0(self):
        S = self.S
        cf = self.cf
        self.rmsnorm(0)
        self.use_scratch("l0")
        pbuf, L_pbuf = self.pbuf, self.L_pbuf
        T = SEQ

        def evac_to(pi):
            def evac(t, bi):
                ts = slice(t * 512, (t + 1) * 512)
                S.op("act", lambda e: e.activation(out=pbuf[pi][:, ts], in_=self.bank_ap(bi), func=AF.Copy),
                     reads=[self.L_bank[bi]], writes=[L_pbuf[pi][t]])
            return evac

        def full(pi):
            return L_pbuf[pi]

        for g in range(4):
            wdw = 2 ** (g + 1)
            a, x1, x2 = g % 4, (g + 1) % 4, (g + 2) % 4
            self.proj_fm(self.xn, self.L_xn, evac_to(a))
            src = a
            d = 1
            dst_cycle = [x1, x2]
            k = 0
            while d < wdw:
                dst = dst_cycle[k % 2]
                k += 1
                S.op("dve", (lambda src, dst, d: (lambda e: e.tensor_tensor(
                    out=pbuf[dst][:, d:T], in0=pbuf[src][:, d:T], in1=pbuf[src][:, 0:T - d], op=ALU.add)))(src, dst, d),
                     reads=full(src), writes=full(dst))
                S.op("dve", (lambda src, dst, d: (lambda e: e.tensor_copy(out=pbuf[dst][:, 0:d], in_=pbuf[src][:, 0:d])))(src, dst, d),
                     reads=[L_pbuf[src][0]], writes=[L_pbuf[dst][0]])
                src = dst
                d *= 2
            S.op("dve", (lambda src, a, wdw: (lambda e: e.scalar_tensor_tensor(
                out=self.pb[:, :], in0=pbuf[src][:, :], scalar=1.0 / wdw, in1=pbuf[a][:, :],
                op0=ALU.mult, op1=ALU.subtract)))(src, a, wdw),
                 reads=full(src) + full(a), writes=self.L_pb)
            tmp = x1 if src != x1 else x2
            S.op("dve", (lambda src, tmp, g: (lambda e: e.tensor_tensor(
                out=pbuf[tmp][:, 0:16], in0=pbuf[src][:, 0:16], in1=cf[:, 64 + g * 16:64 + g * 16 + 16],
                op=ALU.mult)))(src, tmp, g),
                 reads=[L_pbuf[src][0], self.L_cf], writes=[L_pbuf[tmp][0]])
            S.op("dve", (lambda tmp, a: (lambda e: e.tensor_tensor(
                out=self.pb[:, 0:16], in0=pbuf[tmp][:, 0:16], in1=pbuf[a][:, 0:16], op=ALU.subtract)))(tmp, a),
                 reads=[L_pbuf[tmp][0], L_pbuf[a][0]], writes=[self.L_pb[0]])
            for t in range(4):
                ts = slice(t * 512, (t + 1) * 512)
                bi = self.bank()
                self.mm(self.bank_ap(bi), [(self.cb[:, 640 + g * 128:640 + (g + 1) * 128], self.pb[:, ts])],
                        reads=[self.L_cb, self.L_pb[t]], writes=[self.L_bank[bi]])
                S.op("act", (lambda ts, bi, g: (lambda e: e.activation(
                    out=self.mix[:, g, ts], in_=self.bank_ap(bi), func=AF.Copy, scale=cf[:, 40 + g:41 + g])))(ts, bi, g),
                     reads=[self.L_bank[bi], self.L_cf], writes=[self.L_mix[g][t]])
        for cc in range(4):
            b_xb, b_gc, b_gb, b_y = 0, 1, 2, 3
            self.proj_fm(self.xn, self.L_xn, evac_to(b_xb))
            self.proj_fm(self.xn, self.L_xn, evac_to(b_gc))
            S.op("dve", lambda e: e.tensor_tensor(out=pbuf[b_xb][:, :], in0=pbuf[b_xb][:, :], in1=pbuf[b_gc][:, :],
                                                  op=ALU.mult),
                 reads=full(b_xb) + full(b_gc), writes=full(b_xb))
            w0 = cf[:, 44 + 0 * 4 + cc:45 + 0 * 4 + cc]
            w1 = cf[:, 44 + 1 * 4 + cc:45 + 1 * 4 + cc]
            w2 = cf[:, 44 + 2 * 4 + cc:45 + 2 * 4 + cc]
            bb = cf[:, 56 + cc:57 + cc]
            S.op("dve", (lambda w2, bb: (lambda e: e.tensor_scalar(out=pbuf[b_y][:, :], in0=pbuf[b_xb][:, :],
                                                                  scalar1=w2, scalar2=bb, op0=ALU.mult, op1=ALU.add)))(w2, bb),
                 reads=full(b_xb) + [self.L_cf], writes=full(b_y))
            S.op("dve", (lambda w1: (lambda e: e.scalar_tensor_tensor(
                out=pbuf[b_y][:, 1:T], in0=pbuf[b_xb][:, 0:T - 1], scalar=w1, in1=pbuf[b_y][:, 1:T],
                op0=ALU.mult, op1=ALU.add)))(w1),
                 reads=full(b_xb) + full(b_y) + [self.L_cf], writes=full(b_y))
            S.op("dve", (lambda w0: (lambda e: e.scalar_tensor_tensor(
                out=pbuf[b_y][:, 2:T], in0=pbuf[b_xb][:, 0:T - 2], scalar=w0, in1=pbuf[b_y][:, 2:T],
                op0=ALU.mult, op1=ALU.add)))(w0),
                 reads=full(b_xb) + full(b_y) + [self.L_cf], writes=full(b_y))
            self.proj_fm(self.xn, self.L_xn, evac_to(b_gb))
            for t in range(4):
                ts = slice(t * 512, (t + 1) * 512)
                S.op("dve", (lambda ts, cc: (lambda e: e.tensor_tensor(
                    out=self.mix[:, 4 + cc, ts], in0=pbuf[b_y][:, ts], in1=pbuf[b_gb][:, ts], op=ALU.mult)))(ts, cc),
                     reads=[L_pbuf[b_y][t], L_pbuf[b_gb][t]], writes=[self.L_mix[4 + cc][t]])
        self.out_proj()
        self.mlp(0)

    def layer1(self):
        S = self.S
        cf = self.cf
        self.rmsnorm(16)
        self.use_scratch("sgu")
        S.op("pool", lambda e: e.dma_start(out=self.wmv[:, :, :], in_=self.wmov[0].rearrange("p (k n) -> p k n", k=8)),
             writes=[self.L_wmv], dma_key="wmv")
        for g in range(4):
            def evac(t, bi, g=g):
                ts = slice(t * 512, (t + 1) * 512)
                S.op("act", lambda e: e.activation(out=self.mix[:, g, ts], in_=self.bank_ap(bi), func=AF.Gelu),
                     reads=[self.L_bank[bi]], writes=[self.L_mix[g][t]])
            self.proj_fm(self.xn, self.L_xn, evac)
        gbc = cf[:, 128:640]
        bbc = cf[:, 640:1152]
        bsb = cf[:, 1152:1664]
        for n in range(16):
            t = n // 4
            ns = slice(n * 128, (n + 1) * 128)
            i = n % 2
            bi = self.bank()
            pairs = [(self.xn[:, k, ns], self.wmv[:, k, :]) for k in range(8)]
            self.mm(self.bank_ap(bi), pairs, reads=[self.L_wmv] + [self.L_xn[k][t] for k in range(8)],
                    writes=[self.L_bank[bi]])
            vt, vnb, st6 = self.vt[i], self.vnb[i], self.st6
            S.op("act", (lambda bi, vt: (lambda e: e.activation(out=vt[:], in_=self.bank_ap(bi), func=AF.Gelu)))(bi, vt),
                 reads=[self.L_bank[bi]], writes=[self.L_vt[i]])
            S.op("dve", (lambda vt, i: (lambda e: e.bn_stats(out=st6[:, i, 0:6], in_=vt[:])))(vt, i),
                 reads=[self.L_vt[i]], writes=[self.L_st6[i]])
            S.op("dve", (lambda i: (lambda e: e.bn_aggr(out=st6[:, i, 6:8], in_=st6[:, i, 0:6])))(i),
                 reads=[self.L_st6[i]], writes=[self.L_st6[i]])
            S.op("dve", (lambda i: (lambda e: e.tensor_scalar(out=st6[:, i, 7:8], in0=st6[:, i, 7:8], scalar1=EPS,
                                                             scalar2=None, op0=ALU.add)))(i),
                 reads=[self.L_st6[i]], writes=[self.L_st6[i]])
            S.op("pool", (lambda i: (lambda e: e.tensor_tensor(out=st6[:, i, 7:8], in0=st6[:, i, 7:8],
                                                              in1=cf[:, 1664:1665], op=ALU.pow)))(i),
                 reads=[self.L_st6[i], self.L_cf], writes=[self.L_st6[i]])
            S.op("dve", (lambda vt, i: (lambda e: e.tensor_scalar(out=vt[:], in0=vt[:], scalar1=st6[:, i, 6:7],
                                                                 scalar2=st6[:, i, 7:8], op0=ALU.subtract,
                                                                 op1=ALU.mult)))(vt, i),
                 reads=[self.L_vt[i], self.L_st6[i]], writes=[self.L_vt[i]])
            S.op("dve", (lambda vt: (lambda e: e.tensor_tensor(out=vt[:], in0=vt[:], in1=gbc, op=ALU.mult)))(vt),
                 reads=[self.L_vt[i], self.L_cf], writes=[self.L_vt[i]])
            S.op("dve", (lambda vt, vnb: (lambda e: e.tensor_tensor(out=vnb[:], in0=vt[:], in1=bbc, op=ALU.add)))(vt, vnb),
                 reads=[self.L_vt[i], self.L_cf], writes=[self.L_vnb[i]])
            bj = self.bank()

            def fn(pe, bj=bj, vnb=vnb):
                ins = None
                for g in range(4):
                    ins = pe.matmul(self.bank_ap(bj)[:, g * 128:(g + 1) * 128], lhsT=vnb[:, g * 128:(g + 1) * 128],
                                    rhs=self.wmt[:, g * 128:(g + 1) * 128], start=True, stop=True,
                                    skip_group_check=True)
                return ins
            S.op("pe", fn, reads=[self.L_vnb[i], self.L_wmt], writes=[self.L_bank[bj]])
            S.op("dve", (lambda bj, vt: (lambda e: e.tensor_tensor(out=vt[:], in0=self.bank_ap(bj), in1=bsb,
                                                                  op=ALU.add)))(bj, vt),
                 reads=[self.L_bank[bj], self.L_cf], writes=[self.L_vt[i]])
            S.op("dve", (lambda vt, ns: (lambda e: e.tensor_tensor(
                out=self.mix[:, 0:4, ns], in0=vt[:].rearrange("p (g t) -> p g t", g=4), in1=self.mix[:, 0:4, ns],
                op=ALU.mult)))(vt, ns),
                 reads=[self.L_vt[i]] + [self.L_mix[g][t] for g in range(4)],
                 writes=[self.L_mix[g][t] for g in range(4)])
        self.use_scratch("att")
        S.op("pool", lambda e: e.dma_start(out=self.wmv[:, :, :], in_=self.wmov[1].rearrange("p (k n) -> p k n", k=8)),
             writes=[self.L_wmv], dma_key="wmv")
        ident = self.cb[:, 128:256]
        nui = self.cb[:, 256:384]
        nones = self.cb[:, 384:512]
        negm = self.cb[:, 512:640]
        ZB = [0, 1]
        AVB = [4, 5]
        self.bank_rr = 6

        def gbank():
            i = self.bank_rr
            self.bank_rr = 6 + (self.bank_rr - 6 + 1) % 2
            return i

        for a in range(4):
            for which, dst, Ld in ((0, self.qT, self.L_qT), (1, self.kT, self.L_kT)):
                slot, Lr = self.next_block()
                for t in range(4):
                    ts = slice(t * 512, (t + 1) * 512)
                    bi = gbank()
                    pairs = [(self.wring[:, slot, k * 128:(k + 1) * 128], self.xn[:, k, ts]) for k in range(8)]
                    self.mm(self.bank_ap(bi), pairs, reads=[Lr] + [self.L_xn[k][t] for k in range(8)],
                            writes=[self.L_bank[bi]])
                    sc = 0.125 if which == 0 else 1.0
                    S.op("dve", (lambda dst, ts, bi, sc: (lambda e: e.tensor_scalar(
                        out=dst[:, ts], in0=self.bank_ap(bi), scalar1=sc, scalar2=None, op0=ALU.mult)))(dst, ts, bi, sc),
                         reads=[self.L_bank[bi]], writes=[Ld[t]])
                self.prefetch()
            for n4 in range(4):
                bi = gbank()

                def fnv(pe, bi=bi, n4=n4, a=a):
                    ins = None
                    for nn in range(4):
                        n = n4 * 4 + nn
                        for k in range(8):
                            ins = pe.matmul(self.bank_ap(bi)[:, nn * 128:(nn + 1) * 128],
                                            lhsT=self.xn[:, k, n * 128:(n + 1) * 128],
                                            rhs=self.wmv[:, k, a * 128:(a + 1) * 128],
                                            start=(k == 0), stop=(k == 7), skip_group_check=True)
                    return ins
                S.op("pe", fnv, reads=[self.L_wmv] + [self.L_xn[k][n4] for k in range(8)], writes=[self.L_bank[bi]])
                S.op("dve", (lambda bi, n4: (lambda e: e.tensor_copy(
                    out=self.Va[:, n4 * 4:(n4 + 1) * 4, :],
                    in_=self.bank_ap(bi).rearrange("p (n d) -> p n d", n=4))))(bi, n4),
                     reads=[self.L_bank[bi]], writes=[self.L_Va])
            seqs = [[(3, b) for b in range(15, -1, -1)] + [(0, b) for b in range(3, -1, -1)],
                    [(2, b) for b in range(11, -1, -1)] + [(1, b) for b in range(7, -1, -1)]]
            items = []
            for si in range(20):
                for ln in range(2):
                    for h in range(2):
                        Tq, b = seqs[ln][si]
                        items.append(self.att_item(a, ln, h, Tq, b, ZB[ln], AVB[ln]))
            N = len(items)
            for n in range(N + 4):
                if n < N:
                    items[n][0]()
                if 0 <= n - 2 < N:
                    items[n - 2][1]()
                if 0 <= n - 4 < N:
                    items[n - 4][2]()
        self.bank_rr = 0
        self.out_proj()
        self.mlp(1)

    def att_item(self, a, ln, h, Tq, b, zi, avb):
        S = self.S
        ident = self.cb[:, 128:256]
        nui = self.cb[:, 256:384]
        nones = self.cb[:, 384:512]
        negm = self.cb[:, 512:640]
        first = (b == 4 * Tq + 3)
        diag = (b >= 4 * Tq)
        c0 = max(0, b - 4 * Tq) * 128
        q0 = Tq * 512 + c0
        q1 = (Tq + 1) * 512
        zb = 2 * zi + h
        zp = self.bank_ap(zb)
        L_z = [self.L_bank[zb]]
        E, Lp, Cs, Wt = self.E[ln], self.Lp[ln], self.Cs[ln], self.Wt[ln]
        li = ln * 2 + h
        L_E, L_Lp, L_Cs, L_Wt = self.L_E4[li], self.L_Lp4[li], self.L_Cs4[li], self.L_Wt4[li]
        L_av = self.L_av[li]
        ks = slice(b * 128, (b + 1) * 128)
        L_q = [self.L_qT[Tq]]
        L_k = [self.L_kT[b // 4]]
        hp = slice(h * 64, (h + 1) * 64)

        def zmm(pe, with_t):
            out = zp[:, c0:512]
            seq = [(self.kT[hp, ks], self.qT[hp, q0:q1])]
            if with_t:
                seq.append((nui, Lp[:, h, c0:512]))
                if not first:
                    seq.append((nones, Cs[:, h, c0:512]))
            n = len(seq) + (1 if diag else 0)
            ins = None
            for i, (l, r) in enumerate(seq):
                ins = pe.matmul(out, lhsT=l, rhs=r, start=(i == 0), stop=(i == n - 1), skip_group_check=True)
            if diag:
                ins = pe.matmul(zp[:, c0:c0 + 128], lhsT=ident, rhs=negm, start=False, stop=True,
                                skip_group_check=True)
            return ins

        def st1():
            S.op("pe", lambda pe: zmm(pe, False), reads=L_q + L_k + [self.L_cb], writes=L_z)
            S.op("act", lambda e: e.activation(out=E[:, h, c0:512], in_=zp[:, c0:512], func=AF.Exp),
                 reads=L_z, writes=[L_E])
            S.op("act", lambda e: e.activation(out=Lp[:, h, c0:512], in_=E[:, h, c0:512], func=AF.Ln, bias=1.0),
                 reads=[L_E], writes=[L_Lp])

        def st2():
            rd = L_q + L_k + [self.L_cb, L_Lp] + ([] if first else [L_Cs])
            S.op("pe", lambda pe: zmm(pe, True), reads=rd, writes=L_z)
            if first:
                S.op("pool", lambda e: e.memset(Cs[:, h, :], 0.0), writes=[L_Cs])
            if b > 0:
                S.op("pool", lambda e: e.tensor_tensor(out=Cs[:, h, c0:512], in0=Cs[:, h, c0:512],
                                                       in1=Lp[:, h, c0:512], op=ALU.add),
                     reads=[L_Cs, L_Lp], writes=[L_Cs])
            S.op("act", lambda e: e.activation(out=Wt[:, h, c0:512], in_=zp[:, c0:512], func=AF.Exp),
                 reads=L_z, writes=[L_Wt])

        def st3():
            avp = self.bank_ap(avb)
            S.op("pe", lambda pe: pe.matmul(avp[hp, c0:512], lhsT=self.Va[:, b, hp], rhs=Wt[:, h, c0:512],
                                            start=first, stop=(b == 0), skip_group_check=True),
                 reads=[self.L_Va, L_Wt], writes=[L_av, self.L_bank[avb]])
            if b == 0:
                ts = slice(Tq * 512, (Tq + 1) * 512)
                S.op("dve", lambda e: e.tensor_copy(out=self.mix[hp, 4 + a, ts], in_=avp[hp, :]),
                     reads=[L_av, self.L_bank[avb]], writes=[self.L_mix[4 + a][Tq]])

        return (st1, st2, st3)

    def build(self):
        per_seq = 0
        if 0 in self.layers:
            per_seq += 88
        if 1 in self.layers:
            per_seq += 84
        self.blk_total = None
        self.blk_total = self.nseq * NBLK
        self.setup()
        for s in range(self.nseq):
            assert self.blk_use == s * NBLK
            self.load_x(s)
            self.layer0()
            self.layer1()
            self.rmsnorm(32, final_seq=s)
        self.S.emit(final_waits=self.out_ops)
        return self.nc


_CACHE = {}


def kernel(**inputs):
    x = np.asarray(inputs["x"], np.float32)
    wts = _prep_weights({k: np.asarray(v, np.float32) for k, v in inputs.items() if k != "x"})
    if "nc" not in _CACHE:
        _CACHE["nc"] = Builder().build()
    nc = _CACHE["nc"]
    in_maps = []
    for c in range(NCORES):
        xs = np.ascontiguousarray(x[c * NSEQ:(c + 1) * NSEQ].transpose(0, 2, 1))
        m = {"xT": xs}
        m.update(wts)
        in_maps.append(m)
    res = run_bass_kernel_spmd(nc, in_maps, core_ids=list(range(NCORES)))
    out = np.empty((NCORES * NSEQ, SEQ, D), np.float32)
    for c in range(NCORES):
        out[c * NSEQ:(c + 1) * NSEQ] = res.results[c]["oT"].transpose(0, 2, 1)
    return out
```

```python
import contextlib
import numpy as np
import concourse.bass as bass
import concourse.mybir as mybir
from concourse.bass_utils import run_bass_kernel_spmd

F32 = mybir.dt.float32
BF16 = mybir.dt.bfloat16
AF = mybir.ActivationFunctionType
ALU = mybir.AluOpType

NCORES = 8
SEQ = 2048
D = 1024
NSEQ = 2
EPS = 1e-6
RING = 5
NEG = -30000.0

ENGS = ("pe", "act", "dve", "pool", "sp")


class LT:
    __slots__ = ("name", "w", "rs")

    def __init__(self, name):
        self.name = name
        self.w = None
        self.rs = []


class Op:
    __slots__ = ("eng", "fn", "waits", "dma", "chan", "val", "sig", "vc", "idx", "val2")


class Sched:
    def __init__(self, nc):
        self.nc = nc
        self.ops = {e: [] for e in ENGS}
        self.cur = {e: {} for e in ENGS}
        self.dma_gen = {}

    def alias(self, new_lts, old_lts):
        pend = []
        for t in old_lts:
            if t.w is not None:
                pend.append(t.w)
            pend.extend(t.rs)
        for t in new_lts:
            t.rs = list(t.rs) + pend

    def op(self, eng, fn, reads=(), writes=(), dma_key=None):
        o = Op()
        o.eng = eng
        o.fn = fn
        o.dma = dma_key is not None
        o.sig = o.dma
        o.idx = len(self.ops[eng])
        deps = []
        for t in reads:
            if t.w is not None:
                deps.append(t.w)
        for t in writes:
            if t.w is not None:
                deps.append(t.w)
            deps.extend(t.rs)
        cur = self.cur[eng]
        waits = []
        for d in deps:
            if (not d.dma) and (not o.dma) and d.eng == "pe" and eng == "pe":
                continue
            if cur.get(d.chan, -1) >= d.val:
                continue
            waits.append(d)
            d.sig = True
            for k, v in d.vc.items():
                if cur.get(k, -1) < v:
                    cur[k] = v
        best = {}
        for d in waits:
            if d.chan not in best or best[d.chan].val < d.val:
                best[d.chan] = d
        o.waits = list(best.values())
        if o.dma:
            g = self.dma_gen.get(dma_key, 0) + 1
            self.dma_gen[dma_key] = g
            o.chan = ("dma", dma_key)
            o.val = g
        else:
            o.chan = eng
            o.val = o.idx
        vc = dict(cur)
        vc[o.chan] = o.val
        o.vc = vc
        for t in reads:
            t.rs.append(o)
        for t in writes:
            t.w = o
            t.rs = []
        self.ops[eng].append(o)
        return o

    def emit(self, final_waits=()):
        nc = self.nc
        for e in ENGS:
            c = 0
            for o in self.ops[e]:
                if o.dma:
                    continue
                if o.sig:
                    c += 1
                    o.val2 = c
        dma_keys = list(self.dma_gen.keys())
        with contextlib.ExitStack() as st:
            esem = {e: st.enter_context(nc.semaphore("s_" + e)) for e in ENGS}
            dsem = {k: st.enter_context(nc.semaphore("d_%d" % i)) for i, k in enumerate(dma_keys)}
            block = st.enter_context(nc.Block())

            def semval(d):
                if d.dma:
                    return dsem[d.chan[1]], 16 * d.val
                return esem[d.eng], d.val2

            def run(engname):
                def body(eng):
                    for o in self.ops[engname]:
                        for d in o.waits:
                            s, v = semval(d)
                            eng.wait_ge(s, v)
                        ins = o.fn(eng)
                        if o.sig:
                            if o.dma:
                                ins.then_inc(dsem[o.chan[1]], 16)
                            else:
                                ins.then_inc(esem[engname], 1)
                    if engname == "sp":
                        for d in final_waits:
                            s, v = semval(d)
                            eng.wait_ge(s, v)
                return body

            block.tensor(run("pe"))
            block.scalar(run("act"))
            block.vector(run("dve"))
            block.gpsimd(run("pool"))
            block.sync(run("sp"))


def _blk(W, col0):
    return np.ascontiguousarray(
        W[:, col0:col0 + 128].reshape(8, 128, 128).transpose(1, 0, 2)).reshape(128, 1024)


def _mov(W, col0):
    return np.ascontiguousarray(
        W[:, col0:col0 + 512].reshape(8, 128, 512).transpose(1, 0, 2)).reshape(128, 4096)


L0_OC_ORDER = [o for i in range(4) for o in (4 + i, 12 + i, 8 + i, 3 - i)]


def _prep_weights(inp):
    blocks = []
    w = inp["ab_w_in"][0]
    for oc in L0_OC_ORDER:
        blocks.append(_blk(w, oc * 128))
    w = inp["ab_w_out"][0]
    for c in range(8):
        blocks.append(_blk(w, c * 128))

    def mlp_blocks(l):
        w1 = inp["mlp_w1"][l]
        w2 = inp["mlp_w2"][l]
        for q in range(4):
            for j in range(8):
                blocks.append(_blk(w1, (q * 8 + j) * 128))
            for c in range(8):
                blocks.append(_blk(w2[q * 1024:(q + 1) * 1024], c * 128))

    mlp_blocks(0)
    w = inp["cd_w_in"][0]
    for g in range(4):
        blocks.append(_blk(w, g * 128))
    for a in range(4):
        blocks.append(_blk(w, 1024 + a * 128))
        blocks.append(_blk(w, 1536 + a * 128))
    wo = inp["cd_w_out"][0]
    for c in range(8):
        blocks.append(_blk(wo, c * 128))
    mlp_blocks(1)
    wblk = np.ascontiguousarray(np.stack(blocks, 0)).astype(np.float32, copy=False)
    wmov = np.ascontiguousarray(np.stack([_mov(w, 512), _mov(w, 2048)], 0))

    def col8(v):
        return v.reshape(8, 128).T

    gv = np.zeros((128, 64), np.float32)
    gv[:, 0:8] = col8(inp["mix_norm_g"][0])
    gv[:, 8:16] = col8(inp["mlp_norm_g"][0])
    gv[:, 16:24] = col8(inp["mix_norm_g"][1])
    gv[:, 24:32] = col8(inp["mlp_norm_g"][1])
    gv[:, 32:40] = col8(inp["final_norm_g"])
    gv[:, 40:44] = inp["pool_scale"][0].T
    gv[:, 44:56] = inp["conv_w"][0].reshape(3, 4, 128).transpose(2, 0, 1).reshape(128, 12)
    gv[:, 56:60] = inp["conv_b"][0].reshape(4, 128).T
    gv[:, 60] = EPS
    rc = np.zeros((128, 4, 16), np.float32)
    for g, wdw in enumerate((2, 4, 8, 16)):
        rc[:, g, :] = 1.0 / np.minimum(np.arange(16) + 1, wdw)
    cf = np.concatenate([
        gv, rc.reshape(128, 64),
        np.tile(inp["sgu_norm_g"][0].reshape(1, 512), (128, 1)),
        np.tile(inp["sgu_norm_b"][0].reshape(1, 512), (128, 1)),
        np.tile(inp["sgu_b"][0].reshape(1, 512), (128, 1)),
        np.full((128, 512), -0.5, np.float32),
    ], axis=1).astype(np.float32)
    j = np.arange(128)[:, None]
    k = np.arange(128)[None, :]
    ones = np.ones((128, 128), np.float32)
    ident = (j == k).astype(np.float32)
    nui = -(j >= k).astype(np.float32)
    nones = -ones
    negm = np.where(j >= k, NEG, 0.0).astype(np.float32)
    poolw = inp["pool_w"][0].transpose(1, 0, 2).reshape(128, 512)
    cb = np.concatenate([ones, ident, nui, nones, negm, poolw], axis=1).astype(np.float32)
    wT = inp["sgu_w"][0].transpose(2, 0, 1).reshape(128, 512)
    tril = np.tile((j <= k).astype(np.float32), (1, 4))
    sg = np.concatenate([wT, tril], axis=1).astype(np.float32)
    return dict(wblk=wblk, wmov=wmov, cf=cf, cb=cb, sg=sg)


NBLK = 172


class Builder:
    def __init__(self, nseq=NSEQ, layers=(0, 1), debug=False):
        self.nseq = nseq
        self.layers = layers
        nc = self.nc = bass.Bass("TRN2", target_bir_lowering=False)
        self.S = Sched(nc)
        self.xT = nc.dram_tensor("xT", [nseq, D, SEQ], F32, kind="ExternalInput").ap()
        self.wblk = nc.dram_tensor("wblk", [NBLK, 128, 1024], F32, kind="ExternalInput").ap()
        self.wmov = nc.dram_tensor("wmov", [2, 128, 4096], F32, kind="ExternalInput").ap()
        self.cf_d = nc.dram_tensor("cf", [128, 2176], F32, kind="ExternalInput").ap()
        self.cb_d = nc.dram_tensor("cb", [128, 1152], F32, kind="ExternalInput").ap()
        self.sg_d = nc.dram_tensor("sg", [128, 1024], F32, kind="ExternalInput").ap()
        self.oT = nc.dram_tensor("oT", [nseq, D, SEQ], F32, kind="ExternalOutput").ap()
        self.off = 16512
        self.lim = 229344
        self._alloc_fixed()
        self.out_ops = []
        self.blk_next = 0
        self.blk_use = 0
        self.bank_rr = 0

    def sb(self, name, shape, dt, off=None):
        n = 1
        for s in shape[1:]:
            n *= s
        nbytes = n * (4 if dt == F32 else 2)
        nbytes = (nbytes + 31) // 32 * 32
        if off is None:
            off = self.off
            self.off += nbytes
            assert self.off <= self.lim, (name, self.off)
        else:
            assert off + nbytes <= self.lim, (name, off + nbytes)
        return self.nc.alloc_sbuf_tensor_at(name, list(shape), dt, offset=off), off + nbytes

    def _alloc_fixed(self):
        nc = self.nc
        self.hT, _ = self.sb("hT", [128, 8, SEQ], F32)
        self.xn, _ = self.sb("xn", [128, 8, SEQ], BF16)
        self.mix, _ = self.sb("mix", [128, 8, SEQ], BF16)
        self.wring, _ = self.sb("wring", [128, RING, 1024], BF16)
        self.cf, _ = self.sb("cfs", [128, 2176], F32)
        self.cb, _ = self.sb("cbs", [128, 1152], BF16)
        self.wmt, _ = self.sb("wmt", [128, 512], BF16)
        self.ms, _ = self.sb("ms", [128, 2, 512], F32)
        self.scr0 = self.off
        self.L_h = [[LT("h%d_%d" % (c, t)) for t in range(4)] for c in range(8)]
        self.L_xn = [[LT("xn%d_%d" % (c, t)) for t in range(4)] for c in range(8)]
        self.L_mix = [[LT("mix%d_%d" % (c, t)) for t in range(4)] for c in range(8)]
        self.L_ring = [LT("ring%d" % i) for i in range(RING)]
        self.L_cf = LT("cf")
        self.L_cb = LT("cb")
        self.L_wmt = LT("wmt")
        self.L_ms = [LT("ms0"), LT("ms1")]
        self.ps = [nc.alloc_psum_tensor("ps%d" % i, [128, 1024], F32) for i in range(4)]
        self.L_bank = [LT("bank%d" % i) for i in range(8)]
        o = self.scr0
        self.pbuf = []
        for i in range(6):
            t, o = self.sb("pbuf%d" % i, [128, SEQ], F32, off=o)
            self.pbuf.append(t)
        self.pb, o = self.sb("pb", [128, SEQ], BF16, off=o)
        self.L_pbuf = [[LT("pbuf%d_%d" % (i, t)) for t in range(4)] for i in range(6)]
        self.L_pb = [LT("pb_%d" % t) for t in range(4)]
        o = self.scr0
        self.rt = []
        for i in range(3):
            t, o = self.sb("rt%d" % i, [128, 512], F32, off=o)
            self.rt.append(t)
        self.L_rt = [LT("rt%d" % i) for i in range(3)]
        o = self.scr0
        self.wmv, o = self.sb("wmv", [128, 8, 512], BF16, off=o)
        self.L_wmv = LT("wmv")
        o1 = o
        self.vt = []
        self.vnb = []
        for i in range(4):
            t, o = self.sb("vt%d" % i, [128, 512], F32, off=o)
            self.vt.append(t)
        for i in range(4):
            t, o = self.sb("vnb%d" % i, [128, 512], BF16, off=o)
            self.vnb.append(t)
        self.st6, o = self.sb("st6", [128, 4, 8], F32, off=o)
        self.sgst, o = self.sb("sgst", [128, 1024], F32, off=o)
        self.L_vt = [LT("vt%d" % i) for i in range(4)]
        self.L_vnb = [LT("vnb%d" % i) for i in range(4)]
        self.L_st6 = [LT("st6_%d" % i) for i in range(4)]
        self.L_sgst = LT("sgst")
        o = o1
        self.qT = [None, None]
        self.Va = [None, None]
        self.qT[0], o = self.sb("qT0", [128, SEQ], BF16, off=o)
        self.qT[1], o = self.sb("qT1", [128, SEQ], BF16, off=o)
        self.kT, o = self.sb("kT", [128, SEQ], BF16, off=o)
        self.Va[0], o = self.sb("Va0", [128, 16, 128], BF16, off=o)
        self.Va[1], o = self.sb("Va1", [128, 16, 128], BF16, off=o)
        self.E = []
        self.Lp = []
        self.Cs = []
        self.Wt = []
        for l in range(2):
            t, o = self.sb("E%d" % l, [128, 2, 512], F32, off=o)
            self.E.append(t)
        for l in range(2):
            t, o = self.sb("Lp%d" % l, [128, 2, 512], BF16, off=o)
            self.Lp.append(t)
        for l in range(2):
            t, o = self.sb("Cs%d" % l, [128, 2, 512], BF16, off=o)
            self.Cs.append(t)
        for l in range(2):
            t, o = self.sb("Wt%d" % l, [128, 2, 512], BF16, off=o)
            self.Wt.append(t)
        self.L_qT = [LT("qT%d" % t) for t in range(4)]
        self.L_kT = [LT("kT%d" % t) for t in range(4)]
        self.L_Va = LT("Va")
        self.L_E = [LT("E0"), LT("E1")]
        self.L_Lp = [LT("Lp0"), LT("Lp1")]
        self.L_Cs = [LT("Cs0"), LT("Cs1")]
        self.L_Wt = [LT("Wt0"), LT("Wt1")]
        self.L_E4 = [LT("E4_%d" % i) for i in range(4)]
        self.L_Lp4 = [LT("Lp4_%d" % i) for i in range(4)]
        self.L_Cs4 = [LT("Cs4_%d" % i) for i in range(4)]
        self.L_Wt4 = [LT("Wt4_%d" % i) for i in range(4)]
        self.L_av = [LT("av_%d" % i) for i in range(4)]
        self.scr_groups = {
            "l0": [x for r in self.L_pbuf for x in r] + self.L_pb,
            "rt": self.L_rt,
            "sgu": [self.L_wmv] + self.L_vt + self.L_vnb + self.L_st6 + [self.L_sgst],
            "att": [self.L_wmv] + self.L_qT + self.L_kT + [self.L_Va] + self.L_E4 + self.L_Lp4
                   + self.L_Cs4 + self.L_Wt4,
        }
        self.scr_cur = None

    def use_scratch(self, grp):
        if self.scr_cur == grp:
            return
        cur = self.scr_groups[grp]
        old = []
        for k, v in self.scr_groups.items():
            if k != grp:
                old.extend(t for t in v if t not in cur)
        keep = self.scr_groups[self.scr_cur] if self.scr_cur is not None else []
        self.S.alias([t for t in cur if t not in keep], old)
        self.scr_cur = grp

    def bank(self):
        i = self.bank_rr
        self.bank_rr = (self.bank_rr + 1) % 8
        return i

    def bank_ap(self, i):
        return self.ps[i // 2][:, (i % 2) * 512:(i % 2) * 512 + 512]

    def prefetch(self):
        while self.blk_next < self.blk_use + RING and self.blk_next < self.blk_total:
            gb = self.blk_next
            slot = gb % RING
            b = gb % NBLK
            self.S.op("pool",
                      (lambda slot, b: (lambda e: e.dma_start(out=self.wring[:, slot, :], in_=self.wblk[b])))(slot, b),
                      writes=[self.L_ring[slot]], dma_key=("ring", slot))
            self.blk_next += 1

    def next_block(self):
        self.prefetch()
        gb = self.blk_use
        assert gb < self.blk_next
        self.blk_use += 1
        return gb % RING, self.L_ring[gb % RING]

    def mm(self, out_ap, pairs, reads, writes, skip=False, first=True, last=True):
        def fn(pe):
            n = len(pairs)
            ins = None
            for i, (l, r) in enumerate(pairs):
                kw = {}
                if skip:
                    kw["skip_group_check"] = True
                ins = pe.matmul(out_ap, lhsT=l, rhs=r, start=(first and i == 0),
                                stop=(last and i == n - 1), **kw)
            return ins
        return self.S.op("pe", fn, reads=reads, writes=writes)

    def proj_fm(self, rhs_buf, L_rhs, evac):
        slot, Lr = self.next_block()
        for t in range(4):
            bi = self.bank()
            pairs = [(self.wring[:, slot, k * 128:(k + 1) * 128], rhs_buf[:, k, t * 512:(t + 1) * 512])
                     for k in range(8)]
            self.mm(self.bank_ap(bi), pairs, reads=[Lr] + [L_rhs[k][t] for k in range(8)],
                    writes=[self.L_bank[bi]])
            evac(t, bi)
        self.prefetch()

    def setup(self):
        S = self.S
        S.op("sp", lambda e: e.dma_start(out=self.cf[:], in_=self.cf_d), writes=[self.L_cf], dma_key="cf")
        S.op("pool", lambda e: e.dma_start(out=self.cb[:], in_=self.cb_d), writes=[self.L_cb], dma_key="cb")
        self.use_scratch("sgu")
        S.op("sp", lambda e: e.dma_start(out=self.sgst[:], in_=self.sg_d), writes=[self.L_sgst], dma_key="sgst")
        S.op("dve", lambda e: e.tensor_tensor(out=self.wmt[:], in0=self.sgst[:, 0:512], in1=self.sgst[:, 512:1024],
                                              op=ALU.mult),
             reads=[self.L_sgst], writes=[self.L_wmt])

    def load_x(self, s):
        for c in range(8):
            self.S.op("sp", (lambda c: (lambda e: e.dma_start(out=self.hT[:, c, :],
                                                              in_=self.xT[s, c * 128:(c + 1) * 128, :])))(c),
                      writes=self.L_h[c], dma_key=("h", c))

    def rmsnorm(self, gcol, final_seq=None):
        S = self.S
        hT, ms, cf = self.hT, self.ms, self.cf
        sq = self.mix[:, 0:2, :].rearrange("p a (b n) -> p (a b) n", n=512)
        L_sq = [self.L_mix[c][t] for c in range(2) for t in range(4)]
        ones = self.cb[:, 0:128]
        mhalf = cf[:, 1664:2176]
        if final_seq is not None:
            self.use_scratch("rt")
        for t in range(4):
            ts = slice(t * 512, (t + 1) * 512)
            S.op("act", (lambda ts: (lambda e: e.activation(out=sq[:, :, :], in_=hT[:, :, ts], func=AF.Square)))(ts),
                 reads=[self.L_h[c][t] for c in range(8)], writes=L_sq)
            bi = self.bank()
            self.mm(self.bank_ap(bi), [(ones, sq[:, c, :]) for c in range(8)],
                    reads=L_sq + [self.L_cb], writes=[self.L_bank[bi]])
            m = t % 2
            S.op("act", (lambda bi, m: (lambda e: e.activation(out=ms[:, m, :], in_=self.bank_ap(bi), func=AF.Ln,
                                                               scale=1.0 / D, bias=cf[:, 60:61])))(bi, m),
                 reads=[self.L_bank[bi], self.L_cf], writes=[self.L_ms[m]])
            S.op("act", (lambda m: (lambda e: e.activation(out=ms[:, m, :], in_=ms[:, m, :], func=AF.Exp,
                                                           scale=-0.5)))(m),
                 reads=[self.L_ms[m]], writes=[self.L_ms[m]])
            for c in range(8):
                g = cf[:, gcol + c:gcol + c + 1]
                if final_seq is None:
                    S.op("dve", (lambda c, ts, m, g: (lambda e: e.scalar_tensor_tensor(
                        out=self.xn[:, c, ts], in0=hT[:, c, ts], scalar=g, in1=ms[:, m, :],
                        op0=ALU.mult, op1=ALU.mult)))(c, ts, m, g),
                         reads=[self.L_h[c][t], self.L_ms[m], self.L_cf], writes=[self.L_xn[c][t]])
                else:
                    r = (t * 8 + c) % 3
                    S.op("dve", (lambda c, ts, m, g, r: (lambda e: e.scalar_tensor_tensor(
                        out=self.rt[r][:], in0=hT[:, c, ts], scalar=g, in1=ms[:, m, :],
                        op0=ALU.mult, op1=ALU.mult)))(c, ts, m, g, r),
                         reads=[self.L_h[c][t], self.L_ms[m], self.L_cf], writes=[self.L_rt[r]])
                    o = S.op("sp", (lambda c, ts, r: (lambda e: e.dma_start(
                        out=self.oT[final_seq, c * 128:(c + 1) * 128, ts], in_=self.rt[r][:])))(c, ts, r),
                             reads=[self.L_rt[r]], dma_key=("out", r))
                    self.out_ops.append(o)

    def out_proj(self):
        for c in range(8):
            def evac(t, bi, c=c):
                ts = slice(t * 512, (t + 1) * 512)
                self.S.op("dve", lambda e: e.tensor_tensor(out=self.hT[:, c, ts], in0=self.bank_ap(bi),
                                                           in1=self.hT[:, c, ts], op=ALU.add),
                          reads=[self.L_bank[bi], self.L_h[c][t]], writes=[self.L_h[c][t]])
            self.proj_fm(self.mix, self.L_mix, evac)

    def mlp(self, l):
        S = self.S
        self.rmsnorm(8 + 16 * l)
        self.use_scratch("rt")
        rr = [0]
        for q in range(4):
            for j in range(8):
                def evac(t, bi, j=j):
                    ts = slice(t * 512, (t + 1) * 512)
                    r = rr[0] % 3
                    rr[0] += 1
                    S.op("act", lambda e: e.activation(out=self.rt[r][:], in_=self.bank_ap(bi), func=AF.Relu),
                         reads=[self.L_bank[bi]], writes=[self.L_rt[r]])
                    S.op("act", lambda e: e.activation(out=self.mix[:, j, ts], in_=self.rt[r][:], func=AF.Square),
                         reads=[self.L_rt[r]], writes=[self.L_mix[j][t]])
                self.proj_fm(self.xn, self.L_xn, evac)
            for c in range(8):
                def evac2(t, bi, c=c):
                    ts = slice(t * 512, (t + 1) * 512)
                    S.op("dve", lambda e: e.tensor_tensor(out=self.hT[:, c, ts], in0=self.bank_ap(bi),
                                                          in1=self.hT[:, c, ts], op=ALU.add),
                         reads=[self.L_bank[bi], self.L_h[c][t]], writes=[self.L_h[c][t]])
                self.proj_fm(self.mix, self.L_mix, evac2)

    def layer0(self):
        S = self.S
        cf = self.cf
        self.rmsnorm(0)
        self.use_scratch("l0")
        pbuf, L_pbuf = self.pbuf, self.L_pbuf
        T = SEQ

        def evac_to(pi):
            def evac(t, bi):
                ts = slice(t * 512, (t + 1) * 512)
                S.op("act", lambda e: e.activation(out=pbuf[pi][:, ts], in_=self.bank_ap(bi), func=AF.Copy),
                     reads=[self.L_bank[bi]], writes=[L_pbuf[pi][t]])
            return evac

        def full(pi):
            return L_pbuf[pi]

        b_xb, b_gc, b_gb = 0, 1, 2
        PA, PX1, PX2 = 3, 4, 5

        def pool_w_mm(g):
            for t in range(4):
                ts = slice(t * 512, (t + 1) * 512)
                bi = self.bank()
                self.mm(self.bank_ap(bi), [(self.cb[:, 640 + g * 128:640 + (g + 1) * 128], self.pb[:, ts])],
                        reads=[self.L_cb, self.L_pb[t]], writes=[self.L_bank[bi]])
                S.op("act", (lambda ts, bi, g: (lambda e: e.activation(
                    out=self.mix[:, g, ts], in_=self.bank_ap(bi), func=AF.Copy, scale=cf[:, 40 + g:41 + g])))(ts, bi, g),
                     reads=[self.L_bank[bi], self.L_cf], writes=[self.L_mix[g][t]])

        for i in range(4):
            cc = i
            g = 3 - i
            self.proj_fm(self.xn, self.L_xn, evac_to(b_xb))
            self.proj_fm(self.xn, self.L_xn, evac_to(b_gc))
            S.op("dve", lambda e: e.tensor_tensor(out=pbuf[b_xb][:, :], in0=pbuf[b_xb][:, :], in1=pbuf[b_gc][:, :],
                                                  op=ALU.mult),
                 reads=full(b_xb) + full(b_gc), writes=full(b_xb))
            w0 = cf[:, 44 + 0 * 4 + cc:45 + 0 * 4 + cc]
            w1 = cf[:, 44 + 1 * 4 + cc:45 + 1 * 4 + cc]
            w2 = cf[:, 44 + 2 * 4 + cc:45 + 2 * 4 + cc]
            bb = cf[:, 56 + cc:57 + cc]
            b_y = b_gc
            S.op("dve", (lambda w2, bb: (lambda e: e.tensor_scalar(out=pbuf[b_y][:, :], in0=pbuf[b_xb][:, :],
                                                                  scalar1=w2, scalar2=bb, op0=ALU.mult, op1=ALU.add)))(w2, bb),
                 reads=full(b_xb) + [self.L_cf], writes=full(b_y))
            S.op("dve", (lambda w1: (lambda e: e.scalar_tensor_tensor(
                out=pbuf[b_y][:, 1:T], in0=pbuf[b_xb][:, 0:T - 1], scalar=w1, in1=pbuf[b_y][:, 1:T],
                op0=ALU.mult, op1=ALU.add)))(w1),
                 reads=full(b_xb) + full(b_y) + [self.L_cf], writes=full(b_y))
            S.op("dve", (lambda w0: (lambda e: e.scalar_tensor_tensor(
                out=pbuf[b_y][:, 2:T], in0=pbuf[b_xb][:, 0:T - 2], scalar=w0, in1=pbuf[b_y][:, 2:T],
                op0=ALU.mult, op1=ALU.add)))(w0),
                 reads=full(b_xb) + full(b_y) + [self.L_cf], writes=full(b_y))
            self.proj_fm(self.xn, self.L_xn, evac_to(b_gb))
            for t in range(4):
                ts = slice(t * 512, (t + 1) * 512)
                S.op("dve", (lambda ts, cc: (lambda e: e.tensor_tensor(
                    out=self.mix[:, 4 + cc, ts], in0=pbuf[b_y][:, ts], in1=pbuf[b_gb][:, ts], op=ALU.mult)))(ts, cc),
                     reads=[L_pbuf[b_y][t], L_pbuf[b_gb][t]], writes=[self.L_mix[4 + cc][t]])
            if i > 0:
                pool_w_mm(g + 1)
            wdw = 2 ** (g + 1)
            self.proj_fm(self.xn, self.L_xn, evac_to(PA))
            src = PA
            d = 1
            k = 0
            while d < wdw:
                dst = (PX1, PX2)[k % 2]
                k += 1
                S.op("dve", (lambda src, dst, d: (lambda e: e.tensor_tensor(
                    out=pbuf[dst][:, d:T], in0=pbuf[src][:, d:T], in1=pbuf[src][:, 0:T - d], op=ALU.add)))(src, dst, d),
                     reads=full(src), writes=full(dst))
                S.op("dve", (lambda src, dst, d: (lambda e: e.tensor_copy(out=pbuf[dst][:, 0:d], in_=pbuf[src][:, 0:d])))(src, dst, d),
                     reads=[L_pbuf[src][0]], writes=[L_pbuf[dst][0]])
                src = dst
                d *= 2
            tmp = PX1 if src != PX1 else PX2
            S.op("dve", (lambda src, tmp, g: (lambda e: e.tensor_tensor(
                out=pbuf[tmp][:, 0:16], in0=pbuf[src][:, 0:16], in1=cf[:, 64 + g * 16:64 + g * 16 + 16],
                op=ALU.mult)))(src, tmp, g),
                 reads=[L_pbuf[src][0], self.L_cf], writes=[L_pbuf[tmp][0]])
            S.op("dve", (lambda src, wdw: (lambda e: e.scalar_tensor_tensor(
                out=self.pb[:, :], in0=pbuf[src][:, :], scalar=1.0 / wdw, in1=pbuf[PA][:, :],
                op0=ALU.mult, op1=ALU.subtract)))(src, wdw),
                 reads=full(src) + full(PA), writes=self.L_pb)
            S.op("dve", (lambda tmp: (lambda e: e.tensor_tensor(
                out=self.pb[:, 0:16], in0=pbuf[tmp][:, 0:16], in1=pbuf[PA][:, 0:16], op=ALU.subtract)))(tmp),
                 reads=[L_pbuf[tmp][0], L_pbuf[PA][0]], writes=[self.L_pb[0]])
        pool_w_mm(0)
        self.out_proj()
        self.mlp(0)

    def layer1(self):
        S = self.S
        cf = self.cf
        self.rmsnorm(16)
        self.use_scratch("sgu")
        S.op("pool", lambda e: e.dma_start(out=self.wmv[:, :, :], in_=self.wmov[0].rearrange("p (k n) -> p k n", k=8)),
             writes=[self.L_wmv], dma_key="wmv")
        for g in range(4):
            def evac(t, bi, g=g):
                ts = slice(t * 512, (t + 1) * 512)
                S.op("act", lambda e: e.activation(out=self.mix[:, g, ts], in_=self.bank_ap(bi), func=AF.Gelu),
                     reads=[self.L_bank[bi]], writes=[self.L_mix[g][t]])
            self.proj_fm(self.xn, self.L_xn, evac)
        gbc = cf[:, 128:640]
        bbc = cf[:, 640:1152]
        bsb = cf[:, 1152:1664]

        def sgu_a(n):
            t = n // 4
            ns = slice(n * 128, (n + 1) * 128)
            i = n % 4
            bi = self.bank()
            pairs = [(self.xn[:, k, ns], self.wmv[:, k, :]) for k in range(8)]
            self.mm(self.bank_ap(bi), pairs, reads=[self.L_wmv] + [self.L_xn[k][t] for k in range(8)],
                    writes=[self.L_bank[bi]])
            vt, vnb, st6 = self.vt[i], self.vnb[i], self.st6
            S.op("act", lambda e: e.activation(out=vt[:], in_=self.bank_ap(bi), func=AF.Gelu),
                 reads=[self.L_bank[bi]], writes=[self.L_vt[i]])
            S.op("dve", lambda e: e.bn_stats(out=st6[:, i, 0:6], in_=vt[:]),
                 reads=[self.L_vt[i]], writes=[self.L_st6[i]])
            S.op("dve", lambda e: e.bn_aggr(out=st6[:, i, 6:8], in_=st6[:, i, 0:6]),
                 reads=[self.L_st6[i]], writes=[self.L_st6[i]])
            S.op("dve", lambda e: e.tensor_scalar(out=st6[:, i, 7:8], in0=st6[:, i, 7:8], scalar1=EPS,
                                                  scalar2=None, op0=ALU.add),
                 reads=[self.L_st6[i]], writes=[self.L_st6[i]])
            S.op("pool", lambda e: e.tensor_tensor(out=st6[:, i, 7:8], in0=st6[:, i, 7:8],
                                                   in1=cf[:, 1664:1665], op=ALU.pow),
                 reads=[self.L_st6[i], self.L_cf], writes=[self.L_st6[i]])
            S.op("dve", lambda e: e.tensor_scalar(out=vt[:], in0=vt[:], scalar1=st6[:, i, 6:7],
                                                  scalar2=st6[:, i, 7:8], op0=ALU.subtract, op1=ALU.mult),
                 reads=[self.L_vt[i], self.L_st6[i]], writes=[self.L_vt[i]])
            S.op("dve", lambda e: e.tensor_tensor(out=vt[:], in0=vt[:], in1=gbc, op=ALU.mult),
                 reads=[self.L_vt[i], self.L_cf], writes=[self.L_vt[i]])
            S.op("dve", lambda e: e.tensor_tensor(out=vnb[:], in0=vt[:], in1=bbc, op=ALU.add),
                 reads=[self.L_vt[i], self.L_cf], writes=[self.L_vnb[i]])

        def sgu_b(n):
            t = n // 4
            ns = slice(n * 128, (n + 1) * 128)
            i = n % 4
            vt, vnb = self.vt[i], self.vnb[i]
            bj = self.bank()

            def fn(pe):
                ins = None
                for g in range(4):
                    ins = pe.matmul(self.bank_ap(bj)[:, g * 128:(g + 1) * 128], lhsT=vnb[:, g * 128:(g + 1) * 128],
                                    rhs=self.wmt[:, g * 128:(g + 1) * 128], start=True, stop=True,
                                    skip_group_check=True)
                return ins
            S.op("pe", fn, reads=[self.L_vnb[i], self.L_wmt], writes=[self.L_bank[bj]])
            S.op("dve", lambda e: e.tensor_tensor(out=vt[:], in0=self.bank_ap(bj), in1=bsb, op=ALU.add),
                 reads=[self.L_bank[bj], self.L_cf], writes=[self.L_vt[i]])
            S.op("dve", lambda e: e.tensor_tensor(
                out=self.mix[:, 0:4, ns], in0=vt[:].rearrange("p (g t) -> p g t", g=4), in1=self.mix[:, 0:4, ns],
                op=ALU.mult),
                 reads=[self.L_vt[i]] + [self.L_mix[g][t] for g in range(4)],
                 writes=[self.L_mix[g][t] for g in range(4)])

        for n in range(16 + 2):
            if n < 16:
                sgu_a(n)
            if n >= 2:
                sgu_b(n - 2)
        self.use_scratch("att")
        S.op("pool", lambda e: e.dma_start(out=self.wmv[:, :, :], in_=self.wmov[1].rearrange("p (k n) -> p k n", k=8)),
             writes=[self.L_wmv], dma_key="wmv")
        S.op("pool", lambda e: e.memset(self.qT[0][64:128, :], 0.0), writes=self.L_qT)
        S.op("pool", lambda e: e.memset(self.qT[1][0:64, :], 0.0), writes=self.L_qT)
        S.op("pool", lambda e: e.memset(self.Va[0][:, :, 64:128], 0.0), writes=[self.L_Va])
        S.op("pool", lambda e: e.memset(self.Va[1][:, :, 0:64], 0.0), writes=[self.L_Va])
        ZB = [0, 1]
        AVB = [4, 5]
        self.bank_rr = 6

        def gbank():
            i = self.bank_rr
            self.bank_rr = 6 + (self.bank_rr - 6 + 1) % 2
            return i

        for a in range(4):
            for which in (0, 1):
                slot, Lr = self.next_block()
                for t in range(4):
                    ts = slice(t * 512, (t + 1) * 512)
                    bi = gbank()
                    pairs = [(self.wring[:, slot, k * 128:(k + 1) * 128], self.xn[:, k, ts]) for k in range(8)]
                    self.mm(self.bank_ap(bi), pairs, reads=[Lr] + [self.L_xn[k][t] for k in range(8)],
                            writes=[self.L_bank[bi]])
                    if which == 0:
                        for h in range(2):
                            hp = slice(h * 64, (h + 1) * 64)
                            S.op("dve", (lambda ts, bi, h, hp: (lambda e: e.tensor_scalar(
                                out=self.qT[h][hp, ts], in0=self.bank_ap(bi)[hp, :], scalar1=0.125, scalar2=None,
                                op0=ALU.mult)))(ts, bi, h, hp),
                                 reads=[self.L_bank[bi]], writes=[self.L_qT[t]])
                    else:
                        S.op("dve", (lambda ts, bi: (lambda e: e.tensor_copy(out=self.kT[:, ts], in_=self.bank_ap(bi))))(ts, bi),
                             reads=[self.L_bank[bi]], writes=[self.L_kT[t]])
                self.prefetch()
            for n4 in range(4):
                bi = gbank()

                def fnv(pe, bi=bi, n4=n4, a=a):
                    ins = None
                    for nn in range(4):
                        n = n4 * 4 + nn
                        for k in range(8):
                            ins = pe.matmul(self.bank_ap(bi)[:, nn * 128:(nn + 1) * 128],
                                            lhsT=self.xn[:, k, n * 128:(n + 1) * 128],
                                            rhs=self.wmv[:, k, a * 128:(a + 1) * 128],
                                            start=(k == 0), stop=(k == 7), skip_group_check=True)
                    return ins
                S.op("pe", fnv, reads=[self.L_wmv] + [self.L_xn[k][n4] for k in range(8)], writes=[self.L_bank[bi]])
                for h in range(2):
                    S.op("dve", (lambda bi, n4, h: (lambda e: e.tensor_copy(
                        out=self.Va[h][:, n4 * 4:(n4 + 1) * 4, h * 64:(h + 1) * 64],
                        in_=self.bank_ap(bi).rearrange("p (n d) -> p n d", n=4)[:, :, h * 64:(h + 1) * 64])))(bi, n4, h),
                         reads=[self.L_bank[bi]], writes=[self.L_Va])
            seqs = [[(3, b) for b in range(15, -1, -1)] + [(0, b) for b in range(3, -1, -1)],
                    [(2, b) for b in range(11, -1, -1)] + [(1, b) for b in range(7, -1, -1)]]
            items = []
            for si in range(20):
                for ln in range(2):
                    for h in range(2):
                        Tq, b = seqs[ln][si]
                        items.append(self.att_item(a, ln, h, Tq, b, ZB[ln], AVB[ln]))
            N = len(items)
            for n in range(N + 4):
                if n < N:
                    items[n][0]()
                if 0 <= n - 2 < N:
                    items[n - 2][1]()
                if 0 <= n - 4 < N:
                    items[n - 4][2]()
        self.bank_rr = 0
        self.out_proj()
        self.mlp(1)

    def att_item(self, a, ln, h, Tq, b, zi, avb):
        S = self.S
        ident = self.cb[:, 128:256]
        nui = self.cb[:, 256:384]
        nones = self.cb[:, 384:512]
        negm = self.cb[:, 512:640]
        first = (b == 4 * Tq + 3)
        diag = (b >= 4 * Tq)
        c0 = max(0, b - 4 * Tq) * 128
        q0 = Tq * 512 + c0
        q1 = (Tq + 1) * 512
        zb = 2 * zi + h
        zp = self.bank_ap(zb)
        L_z = [self.L_bank[zb]]
        E, Lp, Cs, Wt = self.E[ln], self.Lp[ln], self.Cs[ln], self.Wt[ln]
        li = ln * 2 + h
        L_E, L_Lp, L_Cs, L_Wt = self.L_E4[li], self.L_Lp4[li], self.L_Cs4[li], self.L_Wt4[li]
        ks = slice(b * 128, (b + 1) * 128)
        L_q = [self.L_qT[Tq]]
        L_k = [self.L_kT[b // 4]]
        qTh = self.qT[h]
        Vah = self.Va[h]

        def zmm(pe, with_t):
            out = zp[:, c0:512]
            seq = [(self.kT[:, ks], qTh[:, q0:q1])]
            if with_t:
                seq.append((nui, Lp[:, h, c0:512]))
                if not first:
                    seq.append((nones, Cs[:, h, c0:512]))
            n = len(seq) + (1 if diag else 0)
            ins = None
            for i, (l, r) in enumerate(seq):
                ins = pe.matmul(out, lhsT=l, rhs=r, start=(i == 0), stop=(i == n - 1), skip_group_check=True)
            if diag:
                ins = pe.matmul(zp[:, c0:c0 + 128], lhsT=ident, rhs=negm, start=False, stop=True,
                                skip_group_check=True)
            return ins

        def st1():
            S.op("pe", lambda pe: zmm(pe, False), reads=L_q + L_k + [self.L_cb], writes=L_z)
            S.op("act", lambda e: e.activation(out=E[:, h, c0:512], in_=zp[:, c0:512], func=AF.Exp),
                 reads=L_z, writes=[L_E])
            S.op("act", lambda e: e.activation(out=Lp[:, h, c0:512], in_=E[:, h, c0:512], func=AF.Ln, bias=1.0),
                 reads=[L_E], writes=[L_Lp])

        def st2():
            rd = L_q + L_k + [self.L_cb, L_Lp] + ([] if first else [L_Cs])
            S.op("pe", lambda pe: zmm(pe, True), reads=rd, writes=L_z)
            if first:
                S.op("pool", lambda e: e.memset(Cs[:, h, :], 0.0), writes=[L_Cs])
            if b > 0:
                S.op("pool", lambda e: e.tensor_tensor(out=Cs[:, h, c0:512], in0=Cs[:, h, c0:512],
                                                       in1=Lp[:, h, c0:512], op=ALU.add),
                     reads=[L_Cs, L_Lp], writes=[L_Cs])
            S.op("act", lambda e: e.activation(out=Wt[:, h, c0:512], in_=zp[:, c0:512], func=AF.Exp),
                 reads=L_z, writes=[L_Wt])

        def st3():
            avp = self.bank_ap(avb)
            S.op("pe", lambda pe: pe.matmul(avp[:, c0:512], lhsT=Vah[:, b, :], rhs=Wt[:, h, c0:512],
                                            start=(first and h == 0), stop=(b == 0 and h == 1),
                                            skip_group_check=True),
                 reads=[self.L_Va, L_Wt], writes=[self.L_av[li], self.L_bank[avb]])
            if b == 0 and h == 1:
                ts = slice(Tq * 512, (Tq + 1) * 512)
                S.op("dve", lambda e: e.tensor_copy(out=self.mix[:, 4 + a, ts], in_=avp),
                     reads=[self.L_av[ln * 2], self.L_av[ln * 2 + 1], self.L_bank[avb]],
                     writes=[self.L_mix[4 + a][Tq]])

        return (st1, st2, st3)

    def build(self):
        per_seq = 0
        if 0 in self.layers:
            per_seq += 88
        if 1 in self.layers:
            per_seq += 84
        self.blk_total = None
        self.blk_total = self.nseq * NBLK
        self.setup()
        for s in range(self.nseq):
            assert self.blk_use == s * NBLK
            self.load_x(s)
            self.layer0()
            self.layer1()
            self.rmsnorm(32, final_seq=s)
        self.S.emit(final_waits=self.out_ops)
        return self.nc


_CACHE = {}


def kernel(**inputs):
    x = np.asarray(inputs["x"], np.float32)
    wts = _prep_weights({k: np.asarray(v, np.float32) for k, v in inputs.items() if k != "x"})
    if "nc" not in _CACHE:
        _CACHE["nc"] = Builder().build()
    nc = _CACHE["nc"]
    in_maps = []
    for c in range(NCORES):
        xs = np.ascontiguousarray(x[c * NSEQ:(c + 1) * NSEQ].transpose(0, 2, 1))
        m = {"xT": xs}
        m.update(wts)
        in_maps.append(m)
    res = run_bass_kernel_spmd(nc, in_maps, core_ids=list(range(NCORES)))
    out = np.empty((NCORES * NSEQ, SEQ, D), np.float32)
    for c in range(NCORES):
        out[c * NSEQ:(c + 1) * NSEQ] = res.results[c]["oT"].transpose(0, 2, 1)
    return out
```

```python
import contextlib
import numpy as np
import concourse.bass as bass
import concourse.mybir as mybir
from concourse.bass_utils import run_bass_kernel_spmd

F32 = mybir.dt.float32
BF16 = mybir.dt.bfloat16
AF = mybir.ActivationFunctionType
ALU = mybir.AluOpType

NCORES = 8
SEQ = 2048
D = 1024
NSEQ = 2
EPS = 1e-6
RING = 5
NEG = -30000.0

ENGS = ("pe", "act", "dve", "pool", "sp")


class LT:
    __slots__ = ("name", "w", "rs")

    def __init__(self, name):
        self.name = name
        self.w = None
        self.rs = []


class Op:
    __slots__ = ("eng", "fn", "waits", "dma", "chan", "val", "sig", "vc", "idx", "val2")


class Sched:
    def __init__(self, nc):
        self.nc = nc
        self.ops = {e: [] for e in ENGS}
        self.cur = {e: {} for e in ENGS}
        self.dma_gen = {}

    def alias(self, new_lts, old_lts):
        pend = []
        for t in old_lts:
            if t.w is not None:
                pend.append(t.w)
            pend.extend(t.rs)
        for t in new_lts:
            t.rs = list(t.rs) + pend

    def op(self, eng, fn, reads=(), writes=(), dma_key=None):
        o = Op()
        o.eng = eng
        o.fn = fn
        o.dma = dma_key is not None
        o.sig = o.dma
        o.idx = len(self.ops[eng])
        deps = []
        for t in reads:
            if t.w is not None:
                deps.append(t.w)
        for t in writes:
            if t.w is not None:
                deps.append(t.w)
            deps.extend(t.rs)
        cur = self.cur[eng]
        waits = []
        for d in deps:
            if (not d.dma) and (not o.dma) and d.eng == "pe" and eng == "pe":
                continue
            if cur.get(d.chan, -1) >= d.val:
                continue
            waits.append(d)
            d.sig = True
            for k, v in d.vc.items():
                if cur.get(k, -1) < v:
                    cur[k] = v
        best = {}
        for d in waits:
            if d.chan not in best or best[d.chan].val < d.val:
                best[d.chan] = d
        o.waits = list(best.values())
        if o.dma:
            g = self.dma_gen.get(dma_key, 0) + 1
            self.dma_gen[dma_key] = g
            o.chan = ("dma", dma_key)
            o.val = g
        else:
            o.chan = eng
            o.val = o.idx
        vc = dict(cur)
        vc[o.chan] = o.val
        o.vc = vc
        for t in reads:
            t.rs.append(o)
        for t in writes:
            t.w = o
            t.rs = []
        self.ops[eng].append(o)
        return o

    def emit(self, final_waits=()):
        nc = self.nc
        for e in ENGS:
            c = 0
            for o in self.ops[e]:
                if o.dma:
                    continue
                if o.sig:
                    c += 1
                    o.val2 = c
        dma_keys = list(self.dma_gen.keys())
        with contextlib.ExitStack() as st:
            esem = {e: st.enter_context(nc.semaphore("s_" + e)) for e in ENGS}
            dsem = {k: st.enter_context(nc.semaphore("d_%d" % i)) for i, k in enumerate(dma_keys)}
            block = st.enter_context(nc.Block())

            def semval(d):
                if d.dma:
                    return dsem[d.chan[1]], 16 * d.val
                return esem[d.eng], d.val2

            def run(engname):
                def body(eng):
                    for o in self.ops[engname]:
                        for d in o.waits:
                            s, v = semval(d)
                            eng.wait_ge(s, v)
                        ins = o.fn(eng)
                        if o.sig:
                            if o.dma:
                                ins.then_inc(dsem[o.chan[1]], 16)
                            else:
                                ins.then_inc(esem[engname], 1)
                    if engname == "sp":
                        for d in final_waits:
                            s, v = semval(d)
                            eng.wait_ge(s, v)
                return body

            block.tensor(run("pe"))
            block.scalar(run("act"))
            block.vector(run("dve"))
            block.gpsimd(run("pool"))
            block.sync(run("sp"))


def _blk(W, col0):
    return np.ascontiguousarray(
        W[:, col0:col0 + 128].reshape(8, 128, 128).transpose(1, 0, 2)).reshape(128, 1024)


def _mov(W, col0):
    return np.ascontiguousarray(
        W[:, col0:col0 + 512].reshape(8, 128, 512).transpose(1, 0, 2)).reshape(128, 4096)


L0_OC_ORDER = [o for i in range(4) for o in (4 + i, 12 + i, 8 + i, 3 - i)]


def _prep_weights(inp):
    blocks = []
    w = inp["ab_w_in"][0]
    for oc in L0_OC_ORDER:
        blocks.append(_blk(w, oc * 128))
    w = inp["ab_w_out"][0]
    for c in range(8):
        blocks.append(_blk(w, c * 128))

    def mlp_blocks(l):
        w1 = inp["mlp_w1"][l]
        w2 = inp["mlp_w2"][l]
        for q in range(4):
            for j in range(8):
                blocks.append(_blk(w1, (q * 8 + j) * 128))
            for c in range(8):
                blocks.append(_blk(w2[q * 1024:(q + 1) * 1024], c * 128))

    mlp_blocks(0)
    w = inp["cd_w_in"][0]
    for g in range(4):
        blocks.append(_blk(w, g * 128))
    for a in range(4):
        blocks.append(_blk(w, 1024 + a * 128))
        blocks.append(_blk(w, 1536 + a * 128))
    wo = inp["cd_w_out"][0]
    for c in range(8):
        blocks.append(_blk(wo, c * 128))
    mlp_blocks(1)
    wblk = np.ascontiguousarray(np.stack(blocks, 0)).astype(np.float32, copy=False)
    wmov = np.ascontiguousarray(np.stack([_mov(w, 512), _mov(w, 2048)], 0))

    def col8(v):
        return v.reshape(8, 128).T

    gv = np.zeros((128, 64), np.float32)
    gv[:, 0:8] = col8(inp["mix_norm_g"][0])
    gv[:, 8:16] = col8(inp["mlp_norm_g"][0])
    gv[:, 16:24] = col8(inp["mix_norm_g"][1])
    gv[:, 24:32] = col8(inp["mlp_norm_g"][1])
    gv[:, 32:40] = col8(inp["final_norm_g"])
    gv[:, 40:44] = inp["pool_scale"][0].T
    gv[:, 44:56] = inp["conv_w"][0].reshape(3, 4, 128).transpose(2, 0, 1).reshape(128, 12)
    gv[:, 56:60] = inp["conv_b"][0].reshape(4, 128).T
    gv[:, 60] = EPS
    rc = np.zeros((128, 4, 16), np.float32)
    for g, wdw in enumerate((2, 4, 8, 16)):
        rc[:, g, :] = 1.0 / np.minimum(np.arange(16) + 1, wdw)
    cf = np.concatenate([
        gv, rc.reshape(128, 64),
        np.tile(inp["sgu_norm_g"][0].reshape(1, 512), (128, 1)),
        np.tile(inp["sgu_norm_b"][0].reshape(1, 512), (128, 1)),
        np.tile(inp["sgu_b"][0].reshape(1, 512), (128, 1)),
        np.full((128, 512), -0.5, np.float32),
    ], axis=1).astype(np.float32)
    j = np.arange(128)[:, None]
    k = np.arange(128)[None, :]
    ones = np.ones((128, 128), np.float32)
    ident = (j == k).astype(np.float32)
    nui = -(j >= k).astype(np.float32)
    nones = -ones
    negm = np.where(j >= k, NEG, 0.0).astype(np.float32)
    poolw = inp["pool_w"][0].transpose(1, 0, 2).reshape(128, 512)
    cb = np.concatenate([ones, ident, nui, nones, negm, poolw], axis=1).astype(np.float32)
    wT = inp["sgu_w"][0].transpose(2, 0, 1).reshape(128, 512)
    tril = np.tile((j <= k).astype(np.float32), (1, 4))
    sg = np.concatenate([wT, tril], axis=1).astype(np.float32)
    return dict(wblk=wblk, wmov=wmov, cf=cf, cb=cb, sg=sg)


NBLK = 172


class Builder:
    def __init__(self, nseq=NSEQ, layers=(0, 1), debug=False):
        self.nseq = nseq
        self.layers = layers
        nc = self.nc = bass.Bass("TRN2", target_bir_lowering=False)
        self.S = Sched(nc)
        self.xT = nc.dram_tensor("xT", [nseq, D, SEQ], F32, kind="ExternalInput").ap()
        self.wblk = nc.dram_tensor("wblk", [NBLK, 128, 1024], F32, kind="ExternalInput").ap()
        self.wmov = nc.dram_tensor("wmov", [2, 128, 4096], F32, kind="ExternalInput").ap()
        self.cf_d = nc.dram_tensor("cf", [128, 2176], F32, kind="ExternalInput").ap()
        self.cb_d = nc.dram_tensor("cb", [128, 1152], F32, kind="ExternalInput").ap()
        self.sg_d = nc.dram_tensor("sg", [128, 1024], F32, kind="ExternalInput").ap()
        self.oT = nc.dram_tensor("oT", [nseq, D, SEQ], F32, kind="ExternalOutput").ap()
        self.off = 16512
        self.lim = 229344
        self._alloc_fixed()
        self.out_ops = []
        self.blk_next = 0
        self.blk_use = 0
        self.bank_rr = 0

    def sb(self, name, shape, dt, off=None):
        n = 1
        for s in shape[1:]:
            n *= s
        nbytes = n * (4 if dt == F32 else 2)
        nbytes = (nbytes + 31) // 32 * 32
        if off is None:
            off = self.off
            self.off += nbytes
            assert self.off <= self.lim, (name, self.off)
        else:
            assert off + nbytes <= self.lim, (name, off + nbytes)
        return self.nc.alloc_sbuf_tensor_at(name, list(shape), dt, offset=off), off + nbytes

    def _alloc_fixed(self):
        nc = self.nc
        self.hT, _ = self.sb("hT", [128, 8, SEQ], F32)
        self.xn, _ = self.sb("xn", [128, 8, SEQ], BF16)
        self.mix, _ = self.sb("mix", [128, 8, SEQ], BF16)
        self.wring, _ = self.sb("wring", [128, RING, 1024], BF16)
        self.cf, _ = self.sb("cfs", [128, 2176], F32)
        self.cb, _ = self.sb("cbs", [128, 1152], BF16)
        self.wmt, _ = self.sb("wmt", [128, 512], BF16)
        self.ms, _ = self.sb("ms", [128, 2, 512], F32)
        self.scr0 = self.off
        self.L_h = [[LT("h%d_%d" % (c, t)) for t in range(4)] for c in range(8)]
        self.L_xn = [[LT("xn%d_%d" % (c, t)) for t in range(4)] for c in range(8)]
        self.L_mix = [[LT("mix%d_%d" % (c, t)) for t in range(4)] for c in range(8)]
        self.L_ring = [LT("ring%d" % i) for i in range(RING)]
        self.L_cf = LT("cf")
        self.L_cb = LT("cb")
        self.L_wmt = LT("wmt")
        self.L_ms = [LT("ms0"), LT("ms1")]
        self.ps = [nc.alloc_psum_tensor("ps%d" % i, [128, 1024], F32) for i in range(4)]
        self.L_bank = [LT("bank%d" % i) for i in range(8)]
        o = self.scr0
        self.pbuf = []
        for i in range(6):
            t, o = self.sb("pbuf%d" % i, [128, SEQ], F32, off=o)
            self.pbuf.append(t)
        self.pb, o = self.sb("pb", [128, SEQ], BF16, off=o)
        self.L_pbuf = [[LT("pbuf%d_%d" % (i, t)) for t in range(4)] for i in range(6)]
        self.L_pb = [LT("pb_%d" % t) for t in range(4)]
        o = self.scr0
        self.rt = []
        for i in range(3):
            t, o = self.sb("rt%d" % i, [128, 512], F32, off=o)
            self.rt.append(t)
        self.L_rt = [LT("rt%d" % i) for i in range(3)]
        o = self.scr0
        self.wmv, o = self.sb("wmv", [128, 8, 512], BF16, off=o)
        self.L_wmv = LT("wmv")
        o1 = o
        self.vt = []
        self.vnb = []
        for i in range(4):
            t, o = self.sb("vt%d" % i, [128, 512], F32, off=o)
            self.vt.append(t)
        for i in range(4):
            t, o = self.sb("vnb%d" % i, [128, 512], BF16, off=o)
            self.vnb.append(t)
        self.st6, o = self.sb("st6", [128, 4, 8], F32, off=o)
        self.sgst, o = self.sb("sgst", [128, 1024], F32, off=o)
        self.L_vt = [LT("vt%d" % i) for i in range(4)]
        self.L_vnb = [LT("vnb%d" % i) for i in range(4)]
        self.L_st6 = [LT("st6_%d" % i) for i in range(4)]
        self.L_sgst = LT("sgst")
        o = o1
        self.qT = [None, None]
        self.Va = [None, None]
        self.qT[0], o = self.sb("qT0", [128, SEQ], BF16, off=o)
        self.qT[1], o = self.sb("qT1", [128, SEQ], BF16, off=o)
        self.kT, o = self.sb("kT", [128, SEQ], BF16, off=o)
        self.Va[0], o = self.sb("Va0", [128, 16, 128], BF16, off=o)
        self.Va[1], o = self.sb("Va1", [128, 16, 128], BF16, off=o)
        self.E = []
        self.Lp = []
        self.Cs = []
        self.Wt = []
        for l in range(2):
            t, o = self.sb("E%d" % l, [128, 2, 512], F32, off=o)
            self.E.append(t)
        for l in range(2):
            t, o = self.sb("Lp%d" % l, [128, 2, 512], BF16, off=o)
            self.Lp.append(t)
        for l in range(2):
            t, o = self.sb("Cs%d" % l, [128, 2, 512], BF16, off=o)
            self.Cs.append(t)
        for l in range(2):
            t, o = self.sb("Wt%d" % l, [128, 2, 512], BF16, off=o)
            self.Wt.append(t)
        self.L_qT = [LT("qT%d" % t) for t in range(4)]
        self.L_kT = [LT("kT%d" % t) for t in range(4)]
        self.L_Va = LT("Va")
        self.L_E = [LT("E0"), LT("E1")]
        self.L_Lp = [LT("Lp0"), LT("Lp1")]
        self.L_Cs = [LT("Cs0"), LT("Cs1")]
        self.L_Wt = [LT("Wt0"), LT("Wt1")]
        self.L_E4 = [LT("E4_%d" % i) for i in range(4)]
        self.L_Lp4 = [LT("Lp4_%d" % i) for i in range(4)]
        self.L_Cs4 = [LT("Cs4_%d" % i) for i in range(4)]
        self.L_Wt4 = [LT("Wt4_%d" % i) for i in range(4)]
        self.L_av = [LT("av_%d" % i) for i in range(4)]
        self.scr_groups = {
            "l0": [x for r in self.L_pbuf for x in r] + self.L_pb,
            "rt": self.L_rt,
            "sgu": [self.L_wmv] + self.L_vt + self.L_vnb + self.L_st6 + [self.L_sgst],
            "att": [self.L_wmv] + self.L_qT + self.L_kT + [self.L_Va] + self.L_E4 + self.L_Lp4
                   + self.L_Cs4 + self.L_Wt4,
        }
        self.scr_cur = None

    def use_scratch(self, grp):
        if self.scr_cur == grp:
            return
        cur = self.scr_groups[grp]
        old = []
        for k, v in self.scr_groups.items():
            if k != grp:
                old.extend(t for t in v if t not in cur)
        keep = self.scr_groups[self.scr_cur] if self.scr_cur is not None else []
        self.S.alias([t for t in cur if t not in keep], old)
        self.scr_cur = grp

    def bank(self):
        i = self.bank_rr
        self.bank_rr = (self.bank_rr + 1) % 8
        return i

    def bank_ap(self, i):
        return self.ps[i // 2][:, (i % 2) * 512:(i % 2) * 512 + 512]

    def prefetch(self):
        while self.blk_next < self.blk_use + RING and self.blk_next < self.blk_total:
            gb = self.blk_next
            slot = gb % RING
            b = gb % NBLK
            self.S.op("pool",
                      (lambda slot, b: (lambda e: e.dma_start(out=self.wring[:, slot, :], in_=self.wblk[b])))(slot, b),
                      writes=[self.L_ring[slot]], dma_key=("ring", slot))
            self.blk_next += 1

    def next_block(self):
        self.prefetch()
        gb = self.blk_use
        assert gb < self.blk_next
        self.blk_use += 1
        return gb % RING, self.L_ring[gb % RING]

    def mm(self, out_ap, pairs, reads, writes, skip=False, first=True, last=True):
        def fn(pe):
            n = len(pairs)
            ins = None
            for i, (l, r) in enumerate(pairs):
                kw = {}
                if skip:
                    kw["skip_group_check"] = True
                ins = pe.matmul(out_ap, lhsT=l, rhs=r, start=(first and i == 0),
                                stop=(last and i == n - 1), **kw)
            return ins
        return self.S.op("pe", fn, reads=reads, writes=writes)

    def proj_fm(self, rhs_buf, L_rhs, evac):
        slot, Lr = self.next_block()
        for t in range(4):
            bi = self.bank()
            pairs = [(self.wring[:, slot, k * 128:(k + 1) * 128], rhs_buf[:, k, t * 512:(t + 1) * 512])
                     for k in range(8)]
            self.mm(self.bank_ap(bi), pairs, reads=[Lr] + [L_rhs[k][t] for k in range(8)],
                    writes=[self.L_bank[bi]])
            evac(t, bi)
        self.prefetch()

    def setup(self):
        S = self.S
        S.op("sp", lambda e: e.dma_start(out=self.cf[:], in_=self.cf_d), writes=[self.L_cf], dma_key="cf")
        S.op("pool", lambda e: e.dma_start(out=self.cb[:], in_=self.cb_d), writes=[self.L_cb], dma_key="cb")
        self.use_scratch("sgu")
        S.op("sp", lambda e: e.dma_start(out=self.sgst[:], in_=self.sg_d), writes=[self.L_sgst], dma_key="sgst")
        S.op("dve", lambda e: e.tensor_tensor(out=self.wmt[:], in0=self.sgst[:, 0:512], in1=self.sgst[:, 512:1024],
                                              op=ALU.mult),
             reads=[self.L_sgst], writes=[self.L_wmt])

    def load_x(self, s):
        for t in range(4):
            for c in range(8):
                ts = slice(t * 512, (t + 1) * 512)
                self.S.op("sp", (lambda c, ts: (lambda e: e.dma_start(
                    out=self.hT[:, c, ts], in_=self.xT[s, c * 128:(c + 1) * 128, ts])))(c, ts),
                          writes=[self.L_h[c][t]], dma_key=("h", c, t))

    def rmsnorm(self, gcol, final_seq=None):
        S = self.S
        hT, ms, cf = self.hT, self.ms, self.cf
        sq = self.mix[:, 0:2, :].rearrange("p a (b n) -> p (a b) n", n=512)
        L_sq = [self.L_mix[c][t] for c in range(2) for t in range(4)]
        ones = self.cb[:, 0:128]
        mhalf = cf[:, 1664:2176]
        if final_seq is not None:
            self.use_scratch("rt")
        for t in range(4):
            ts = slice(t * 512, (t + 1) * 512)
            S.op("act", (lambda ts: (lambda e: e.activation(out=sq[:, :, :], in_=hT[:, :, ts], func=AF.Square)))(ts),
                 reads=[self.L_h[c][t] for c in range(8)], writes=L_sq)
            bi = self.bank()
            self.mm(self.bank_ap(bi), [(ones, sq[:, c, :]) for c in range(8)],
                    reads=L_sq + [self.L_cb], writes=[self.L_bank[bi]])
            m = t % 2
            S.op("act", (lambda bi, m: (lambda e: e.activation(out=ms[:, m, :], in_=self.bank_ap(bi), func=AF.Ln,
                                                               scale=1.0 / D, bias=cf[:, 60:61])))(bi, m),
                 reads=[self.L_bank[bi], self.L_cf], writes=[self.L_ms[m]])
            S.op("act", (lambda m: (lambda e: e.activation(out=ms[:, m, :], in_=ms[:, m, :], func=AF.Exp,
                                                           scale=-0.5)))(m),
                 reads=[self.L_ms[m]], writes=[self.L_ms[m]])
            for c in range(8):
                g = cf[:, gcol + c:gcol + c + 1]
                if final_seq is None:
                    S.op("dve", (lambda c, ts, m, g: (lambda e: e.scalar_tensor_tensor(
                        out=self.xn[:, c, ts], in0=hT[:, c, ts], scalar=g, in1=ms[:, m, :],
                        op0=ALU.mult, op1=ALU.mult)))(c, ts, m, g),
                         reads=[self.L_h[c][t], self.L_ms[m], self.L_cf], writes=[self.L_xn[c][t]])
                else:
                    r = (t * 8 + c) % 3
                    S.op("dve", (lambda c, ts, m, g, r: (lambda e: e.scalar_tensor_tensor(
                        out=self.rt[r][:], in0=hT[:, c, ts], scalar=g, in1=ms[:, m, :],
                        op0=ALU.mult, op1=ALU.mult)))(c, ts, m, g, r),
                         reads=[self.L_h[c][t], self.L_ms[m], self.L_cf], writes=[self.L_rt[r]])
                    o = S.op("sp", (lambda c, ts, r: (lambda e: e.dma_start(
                        out=self.oT[final_seq, c * 128:(c + 1) * 128, ts], in_=self.rt[r][:])))(c, ts, r),
                             reads=[self.L_rt[r]], dma_key=("out", r))
                    self.out_ops.append(o)

    def out_proj(self):
        for c in range(8):
            def evac(t, bi, c=c):
                ts = slice(t * 512, (t + 1) * 512)
                self.S.op("dve", lambda e: e.tensor_tensor(out=self.hT[:, c, ts], in0=self.bank_ap(bi),
                                                           in1=self.hT[:, c, ts], op=ALU.add),
                          reads=[self.L_bank[bi], self.L_h[c][t]], writes=[self.L_h[c][t]])
            self.proj_fm(self.mix, self.L_mix, evac)

    def mlp(self, l):
        S = self.S
        self.rmsnorm(8 + 16 * l)
        self.use_scratch("rt")
        rr = [0]
        for q in range(4):
            for j in range(8):
                def evac(t, bi, j=j):
                    ts = slice(t * 512, (t + 1) * 512)
                    r = rr[0] % 3
                    rr[0] += 1
                    S.op("act", lambda e: e.activation(out=self.rt[r][:], in_=self.bank_ap(bi), func=AF.Relu),
                         reads=[self.L_bank[bi]], writes=[self.L_rt[r]])
                    S.op("act", lambda e: e.activation(out=self.mix[:, j, ts], in_=self.rt[r][:], func=AF.Square),
                         reads=[self.L_rt[r]], writes=[self.L_mix[j][t]])
                self.proj_fm(self.xn, self.L_xn, evac)
            for c in range(8):
                def evac2(t, bi, c=c):
                    ts = slice(t * 512, (t + 1) * 512)
                    S.op("dve", lambda e: e.tensor_tensor(out=self.hT[:, c, ts], in0=self.bank_ap(bi),
                                                          in1=self.hT[:, c, ts], op=ALU.add),
                         reads=[self.L_bank[bi], self.L_h[c][t]], writes=[self.L_h[c][t]])
                self.proj_fm(self.mix, self.L_mix, evac2)

    def layer0(self):
        S = self.S
        cf = self.cf
        self.rmsnorm(0)
        self.use_scratch("l0")
        pbuf, L_pbuf = self.pbuf, self.L_pbuf
        T = SEQ

        def evac_to(pi):
            def evac(t, bi):
                ts = slice(t * 512, (t + 1) * 512)
                S.op("act", lambda e: e.activation(out=pbuf[pi][:, ts], in_=self.bank_ap(bi), func=AF.Copy),
                     reads=[self.L_bank[bi]], writes=[L_pbuf[pi][t]])
            return evac

        def full(pi):
            return L_pbuf[pi]

        b_xb, b_gc, b_gb = 0, 1, 2
        PA, PX1, PX2 = 3, 4, 5

        def pool_w_mm(g):
            for t in range(4):
                ts = slice(t * 512, (t + 1) * 512)
                bi = self.bank()
                self.mm(self.bank_ap(bi), [(self.cb[:, 640 + g * 128:640 + (g + 1) * 128], self.pb[:, ts])],
                        reads=[self.L_cb, self.L_pb[t]], writes=[self.L_bank[bi]])
                S.op("act", (lambda ts, bi, g: (lambda e: e.activation(
                    out=self.mix[:, g, ts], in_=self.bank_ap(bi), func=AF.Copy, scale=cf[:, 40 + g:41 + g])))(ts, bi, g),
                     reads=[self.L_bank[bi], self.L_cf], writes=[self.L_mix[g][t]])

        for i in range(4):
            cc = i
            g = 3 - i
            self.proj_fm(self.xn, self.L_xn, evac_to(b_xb))
            self.proj_fm(self.xn, self.L_xn, evac_to(b_gc))
            S.op("dve", lambda e: e.tensor_tensor(out=pbuf[b_xb][:, :], in0=pbuf[b_xb][:, :], in1=pbuf[b_gc][:, :],
                                                  op=ALU.mult),
                 reads=full(b_xb) + full(b_gc), writes=full(b_xb))
            w0 = cf[:, 44 + 0 * 4 + cc:45 + 0 * 4 + cc]
            w1 = cf[:, 44 + 1 * 4 + cc:45 + 1 * 4 + cc]
            w2 = cf[:, 44 + 2 * 4 + cc:45 + 2 * 4 + cc]
            bb = cf[:, 56 + cc:57 + cc]
            b_y = b_gc
            S.op("dve", (lambda w2, bb: (lambda e: e.tensor_scalar(out=pbuf[b_y][:, :], in0=pbuf[b_xb][:, :],
                                                                  scalar1=w2, scalar2=bb, op0=ALU.mult, op1=ALU.add)))(w2, bb),
                 reads=full(b_xb) + [self.L_cf], writes=full(b_y))
            S.op("dve", (lambda w1: (lambda e: e.scalar_tensor_tensor(
                out=pbuf[b_y][:, 1:T], in0=pbuf[b_xb][:, 0:T - 1], scalar=w1, in1=pbuf[b_y][:, 1:T],
                op0=ALU.mult, op1=ALU.add)))(w1),
                 reads=full(b_xb) + full(b_y) + [self.L_cf], writes=full(b_y))
            S.op("dve", (lambda w0: (lambda e: e.scalar_tensor_tensor(
                out=pbuf[b_y][:, 2:T], in0=pbuf[b_xb][:, 0:T - 2], scalar=w0, in1=pbuf[b_y][:, 2:T],
                op0=ALU.mult, op1=ALU.add)))(w0),
                 reads=full(b_xb) + full(b_y) + [self.L_cf], writes=full(b_y))
            self.proj_fm(self.xn, self.L_xn, evac_to(b_gb))
            for t in range(4):
                ts = slice(t * 512, (t + 1) * 512)
                S.op("dve", (lambda ts, cc: (lambda e: e.tensor_tensor(
                    out=self.mix[:, 4 + cc, ts], in0=pbuf[b_y][:, ts], in1=pbuf[b_gb][:, ts], op=ALU.mult)))(ts, cc),
                     reads=[L_pbuf[b_y][t], L_pbuf[b_gb][t]], writes=[self.L_mix[4 + cc][t]])
            if i > 0:
                pool_w_mm(g + 1)
            wdw = 2 ** (g + 1)
            self.proj_fm(self.xn, self.L_xn, evac_to(PA))
            src = PA
            d = 1
            k = 0
            while d < wdw:
                dst = (PX1, PX2)[k % 2]
                k += 1
                S.op("dve", (lambda src, dst, d: (lambda e: e.tensor_tensor(
                    out=pbuf[dst][:, d:T], in0=pbuf[src][:, d:T], in1=pbuf[src][:, 0:T - d], op=ALU.add)))(src, dst, d),
                     reads=full(src), writes=full(dst))
                S.op("dve", (lambda src, dst, d: (lambda e: e.tensor_copy(out=pbuf[dst][:, 0:d], in_=pbuf[src][:, 0:d])))(src, dst, d),
                     reads=[L_pbuf[src][0]], writes=[L_pbuf[dst][0]])
                src = dst
                d *= 2
            tmp = PX1 if src != PX1 else PX2
            S.op("dve", (lambda src, tmp, g: (lambda e: e.tensor_tensor(
                out=pbuf[tmp][:, 0:16], in0=pbuf[src][:, 0:16], in1=cf[:, 64 + g * 16:64 + g * 16 + 16],
                op=ALU.mult)))(src, tmp, g),
                 reads=[L_pbuf[src][0], self.L_cf], writes=[L_pbuf[tmp][0]])
            S.op("dve", (lambda src, wdw: (lambda e: e.scalar_tensor_tensor(
                out=self.pb[:, :], in0=pbuf[src][:, :], scalar=1.0 / wdw, in1=pbuf[PA][:, :],
                op0=ALU.mult, op1=ALU.subtract)))(src, wdw),
                 reads=full(src) + full(PA), writes=self.L_pb)
            S.op("dve", (lambda tmp: (lambda e: e.tensor_tensor(
                out=self.pb[:, 0:16], in0=pbuf[tmp][:, 0:16], in1=pbuf[PA][:, 0:16], op=ALU.subtract)))(tmp),
                 reads=[L_pbuf[tmp][0], L_pbuf[PA][0]], writes=[self.L_pb[0]])
        pool_w_mm(0)
        self.out_proj()
        self.mlp(0)

    def layer1(self):
        S = self.S
        cf = self.cf
        self.rmsnorm(16)
        self.use_scratch("sgu")
        S.op("pool", lambda e: e.dma_start(out=self.wmv[:, :, :], in_=self.wmov[0].rearrange("p (k n) -> p k n", k=8)),
             writes=[self.L_wmv], dma_key="wmv")
        for g in range(4):
            def evac(t, bi, g=g):
                ts = slice(t * 512, (t + 1) * 512)
                S.op("act", lambda e: e.activation(out=self.mix[:, g, ts], in_=self.bank_ap(bi), func=AF.Gelu),
                     reads=[self.L_bank[bi]], writes=[self.L_mix[g][t]])
            self.proj_fm(self.xn, self.L_xn, evac)
        gbc = cf[:, 128:640]
        bbc = cf[:, 640:1152]
        bsb = cf[:, 1152:1664]

        def sgu_a(n):
            t = n // 4
            ns = slice(n * 128, (n + 1) * 128)
            i = n % 4
            bi = self.bank()
            pairs = [(self.xn[:, k, ns], self.wmv[:, k, :]) for k in range(8)]
            self.mm(self.bank_ap(bi), pairs, reads=[self.L_wmv] + [self.L_xn[k][t] for k in range(8)],
                    writes=[self.L_bank[bi]])
            vt, vnb, st6 = self.vt[i], self.vnb[i], self.st6
            S.op("act", lambda e: e.activation(out=vt[:], in_=self.bank_ap(bi), func=AF.Gelu),
                 reads=[self.L_bank[bi]], writes=[self.L_vt[i]])
            S.op("dve", lambda e: e.bn_stats(out=st6[:, i, 0:6], in_=vt[:]),
                 reads=[self.L_vt[i]], writes=[self.L_st6[i]])
            S.op("dve", lambda e: e.bn_aggr(out=st6[:, i, 6:8], in_=st6[:, i, 0:6]),
                 reads=[self.L_st6[i]], writes=[self.L_st6[i]])
            S.op("dve", lambda e: e.tensor_scalar(out=st6[:, i, 7:8], in0=st6[:, i, 7:8], scalar1=EPS,
                                                  scalar2=None, op0=ALU.add),
                 reads=[self.L_st6[i]], writes=[self.L_st6[i]])
            S.op("pool", lambda e: e.tensor_tensor(out=st6[:, i, 7:8], in0=st6[:, i, 7:8],
                                                   in1=cf[:, 1664:1665], op=ALU.pow),
                 reads=[self.L_st6[i], self.L_cf], writes=[self.L_st6[i]])
            S.op("dve", lambda e: e.tensor_scalar(out=vt[:], in0=vt[:], scalar1=st6[:, i, 6:7],
                                                  scalar2=st6[:, i, 7:8], op0=ALU.subtract, op1=ALU.mult),
                 reads=[self.L_vt[i], self.L_st6[i]], writes=[self.L_vt[i]])
            S.op("pool", lambda e: e.tensor_tensor(out=vt[:], in0=vt[:], in1=gbc, op=ALU.mult),
                 reads=[self.L_vt[i], self.L_cf], writes=[self.L_vt[i]])
            S.op("pool", lambda e: e.tensor_tensor(out=vnb[:], in0=vt[:], in1=bbc, op=ALU.add),
                 reads=[self.L_vt[i], self.L_cf], writes=[self.L_vnb[i]])

        def sgu_b(n):
            t = n // 4
            ns = slice(n * 128, (n + 1) * 128)
            i = n % 4
            vt, vnb = self.vt[i], self.vnb[i]
            bj = self.bank()

            def fn(pe):
                ins = None
                for g in range(4):
                    ins = pe.matmul(self.bank_ap(bj)[:, g * 128:(g + 1) * 128], lhsT=vnb[:, g * 128:(g + 1) * 128],
                                    rhs=self.wmt[:, g * 128:(g + 1) * 128], start=True, stop=True,
                                    skip_group_check=True)
                return ins
            S.op("pe", fn, reads=[self.L_vnb[i], self.L_wmt], writes=[self.L_bank[bj]])
            S.op("dve", lambda e: e.tensor_tensor(out=vt[:], in0=self.bank_ap(bj), in1=bsb, op=ALU.add),
                 reads=[self.L_bank[bj], self.L_cf], writes=[self.L_vt[i]])
            S.op("dve", lambda e: e.tensor_tensor(
                out=self.mix[:, 0:4, ns], in0=vt[:].rearrange("p (g t) -> p g t", g=4), in1=self.mix[:, 0:4, ns],
                op=ALU.mult),
                 reads=[self.L_vt[i]] + [self.L_mix[g][t] for g in range(4)],
                 writes=[self.L_mix[g][t] for g in range(4)])

        for n in range(16 + 3):
            if n < 16:
                sgu_a(n)
            if n >= 3:
                sgu_b(n - 3)
        self.use_scratch("att")
        S.op("pool", lambda e: e.dma_start(out=self.wmv[:, :, :], in_=self.wmov[1].rearrange("p (k n) -> p k n", k=8)),
             writes=[self.L_wmv], dma_key="wmv")
        S.op("pool", lambda e: e.memset(self.qT[0][64:128, :], 0.0), writes=self.L_qT)
        S.op("pool", lambda e: e.memset(self.qT[1][0:64, :], 0.0), writes=self.L_qT)
        S.op("pool", lambda e: e.memset(self.Va[0][:, :, 64:128], 0.0), writes=[self.L_Va])
        S.op("pool", lambda e: e.memset(self.Va[1][:, :, 0:64], 0.0), writes=[self.L_Va])
        ZB = [0, 1]
        AVB = [4, 5]
        self.bank_rr = 6

        def gbank():
            i = self.bank_rr
            self.bank_rr = 6 + (self.bank_rr - 6 + 1) % 2
            return i

        for a in range(4):
            for which in (0, 1):
                slot, Lr = self.next_block()
                for t in range(4):
                    ts = slice(t * 512, (t + 1) * 512)
                    bi = gbank()
                    pairs = [(self.wring[:, slot, k * 128:(k + 1) * 128], self.xn[:, k, ts]) for k in range(8)]
                    self.mm(self.bank_ap(bi), pairs, reads=[Lr] + [self.L_xn[k][t] for k in range(8)],
                            writes=[self.L_bank[bi]])
                    if which == 0:
                        for h in range(2):
                            hp = slice(h * 64, (h + 1) * 64)
                            S.op("dve", (lambda ts, bi, h, hp: (lambda e: e.tensor_scalar(
                                out=self.qT[h][hp, ts], in0=self.bank_ap(bi)[hp, :], scalar1=0.125, scalar2=None,
                                op0=ALU.mult)))(ts, bi, h, hp),
                                 reads=[self.L_bank[bi]], writes=[self.L_qT[t]])
                    else:
                        S.op("dve", (lambda ts, bi: (lambda e: e.tensor_copy(out=self.kT[:, ts], in_=self.bank_ap(bi))))(ts, bi),
                             reads=[self.L_bank[bi]], writes=[self.L_kT[t]])
                self.prefetch()
            for n4 in range(4):
                bi = gbank()

                def fnv(pe, bi=bi, n4=n4, a=a):
                    ins = None
                    for nn in range(4):
                        n = n4 * 4 + nn
                        for k in range(8):
                            ins = pe.matmul(self.bank_ap(bi)[:, nn * 128:(nn + 1) * 128],
                                            lhsT=self.xn[:, k, n * 128:(n + 1) * 128],
                                            rhs=self.wmv[:, k, a * 128:(a + 1) * 128],
                                            start=(k == 0), stop=(k == 7), skip_group_check=True)
                    return ins
                S.op("pe", fnv, reads=[self.L_wmv] + [self.L_xn[k][n4] for k in range(8)], writes=[self.L_bank[bi]])
                for h in range(2):
                    S.op("dve", (lambda bi, n4, h: (lambda e: e.tensor_copy(
                        out=self.Va[h][:, n4 * 4:(n4 + 1) * 4, h * 64:(h + 1) * 64],
                        in_=self.bank_ap(bi).rearrange("p (n d) -> p n d", n=4)[:, :, h * 64:(h + 1) * 64])))(bi, n4, h),
                         reads=[self.L_bank[bi]], writes=[self.L_Va])
            seqs = [[(3, b) for b in range(15, -1, -1)] + [(0, b) for b in range(3, -1, -1)],
                    [(2, b) for b in range(11, -1, -1)] + [(1, b) for b in range(7, -1, -1)]]
            items = []
            for si in range(20):
                for ln in range(2):
                    for h in range(2):
                        Tq, b = seqs[ln][si]
                        items.append(self.att_item(a, ln, h, Tq, b, ZB[ln], AVB[ln]))
            N = len(items)
            for n in range(N + 4):
                if n < N:
                    items[n][0]()
                if 0 <= n - 2 < N:
                    items[n - 2][1]()
                if 0 <= n - 4 < N:
                    items[n - 4][2]()
        self.bank_rr = 0
        self.out_proj()
        self.mlp(1)

    def att_item(self, a, ln, h, Tq, b, zi, avb):
        S = self.S
        ident = self.cb[:, 128:256]
        nui = self.cb[:, 256:384]
        nones = self.cb[:, 384:512]
        negm = self.cb[:, 512:640]
        first = (b == 4 * Tq + 3)
        diag = (b >= 4 * Tq)
        c0 = max(0, b - 4 * Tq) * 128
        q0 = Tq * 512 + c0
        q1 = (Tq + 1) * 512
        zb = 2 * zi + h
        zp = self.bank_ap(zb)
        L_z = [self.L_bank[zb]]
        E, Lp, Cs, Wt = self.E[ln], self.Lp[ln], self.Cs[ln], self.Wt[ln]
        li = ln * 2 + h
        L_E, L_Lp, L_Cs, L_Wt = self.L_E4[li], self.L_Lp4[li], self.L_Cs4[li], self.L_Wt4[li]
        ks = slice(b * 128, (b + 1) * 128)
        L_q = [self.L_qT[Tq]]
        L_k = [self.L_kT[b // 4]]
        qTh = self.qT[h]
        Vah = self.Va[h]

        def zmm(pe, with_t):
            out = zp[:, c0:512]
            seq = [(self.kT[:, ks], qTh[:, q0:q1])]
            if with_t:
                seq.append((nui, Lp[:, h, c0:512]))
                if not first:
                    seq.append((nones, Cs[:, h, c0:512]))
            n = len(seq) + (1 if diag else 0)
            ins = None
            for i, (l, r) in enumerate(seq):
                ins = pe.matmul(out, lhsT=l, rhs=r, start=(i == 0), stop=(i == n - 1), skip_group_check=True)
            if diag:
                ins = pe.matmul(zp[:, c0:c0 + 128], lhsT=ident, rhs=negm, start=False, stop=True,
                                skip_group_check=True)
            return ins

        def st1():
            S.op("pe", lambda pe: zmm(pe, False), reads=L_q + L_k + [self.L_cb], writes=L_z)
            S.op("act", lambda e: e.activation(out=zp[:, c0:512], in_=zp[:, c0:512], func=AF.Exp),
                 reads=L_z, writes=L_z)
            S.op("act", lambda e: e.activation(out=Lp[:, h, c0:512], in_=zp[:, c0:512], func=AF.Ln, bias=1.0),
                 reads=L_z, writes=[L_Lp])

        def st2():
            rd = L_q + L_k + [self.L_cb, L_Lp] + ([] if first else [L_Cs])
            S.op("pe", lambda pe: zmm(pe, True), reads=rd, writes=L_z)
            if first:
                S.op("pool", lambda e: e.memset(Cs[:, h, :], 0.0), writes=[L_Cs])
            if b > 0:
                S.op("pool", lambda e: e.tensor_tensor(out=Cs[:, h, c0:512], in0=Cs[:, h, c0:512],
                                                       in1=Lp[:, h, c0:512], op=ALU.add),
                     reads=[L_Cs, L_Lp], writes=[L_Cs])
            S.op("act", lambda e: e.activation(out=Wt[:, h, c0:512], in_=zp[:, c0:512], func=AF.Exp),
                 reads=L_z, writes=[L_Wt])

        def st3():
            avp = self.bank_ap(avb)
            S.op("pe", lambda pe: pe.matmul(avp[:, c0:512], lhsT=Vah[:, b, :], rhs=Wt[:, h, c0:512],
                                            start=(first and h == 0), stop=(b == 0 and h == 1),
                                            skip_group_check=True),
                 reads=[self.L_Va, L_Wt], writes=[self.L_av[li], self.L_bank[avb]])
            if b == 0 and h == 1:
                ts = slice(Tq * 512, (Tq + 1) * 512)
                S.op("dve", lambda e: e.tensor_copy(out=self.mix[:, 4 + a, ts], in_=avp),
                     reads=[self.L_av[ln * 2], self.L_av[ln * 2 + 1], self.L_bank[avb]],
                     writes=[self.L_mix[4 + a][Tq]])

        return (st1, st2, st3)

    def build(self):
        per_seq = 0
        if 0 in self.layers:
            per_seq += 88
        if 1 in self.layers:
            per_seq += 84
        self.blk_total = None
        self.blk_total = self.nseq * NBLK
        self.setup()
        for s in range(self.nseq):
            assert self.blk_use == s * NBLK
            self.load_x(s)
            self.layer0()
            self.layer1()
            self.rmsnorm(32, final_seq=s)
        self.S.emit(final_waits=self.out_ops)
        return self.nc


_CACHE = {}


def kernel(**inputs):
    x = np.asarray(inputs["x"], np.float32)
    wts = _prep_weights({k: np.asarray(v, np.float32) for k, v in inputs.items() if k != "x"})
    if "nc" not in _CACHE:
        _CACHE["nc"] = Builder().build()
    nc = _CACHE["nc"]
    in_maps = []
    for c in range(NCORES):
        xs = np.ascontiguousarray(x[c * NSEQ:(c + 1) * NSEQ].transpose(0, 2, 1))
        m = {"xT": xs}
        m.update(wts)
        in_maps.append(m)
    res = run_bass_kernel_spmd(nc, in_maps, core_ids=list(range(NCORES)))
    out = np.empty((NCORES * NSEQ, SEQ, D), np.float32)
    for c in range(NCORES):
        out[c * NSEQ:(c + 1) * NSEQ] = res.results[c]["oT"].transpose(0, 2, 1)
    return out
```
